# Optimizing a Trainium2 kernel written in Bass

```python
import math
import jax, jax.numpy as jnp
from jax import lax
import numpy as np

D_MODEL = 2048
BATCH = 4
SEQ = 4096
DEPTH = 4

CHUNK = 64
Q_BLOCK = 128
N_MIXERS = 2
N_A = (DEPTH + 1) // 2
N_B = DEPTH // 2

SB_HEADS = 16
SB_HEAD_DIM = D_MODEL // SB_HEADS
SB_IN = 3 * SB_HEADS * SB_HEAD_DIM

DSA_HEADS = 16
DSA_KV_HEADS = 4
DSA_GROUP = DSA_HEADS // DSA_KV_HEADS
DSA_HEAD_DIM = D_MODEL // DSA_HEADS
IDX_HEADS = 16
IDX_HEAD_DIM = 64
IDX_TOPK_MAX = 256
DSA_SIZES = (DSA_HEADS * DSA_HEAD_DIM,
             DSA_KV_HEADS * DSA_HEAD_DIM,
             DSA_KV_HEADS * DSA_HEAD_DIM,
             IDX_HEADS * IDX_HEAD_DIM,
             IDX_HEAD_DIM,
             IDX_HEADS)
DSA_IN = sum(DSA_SIZES)

REL_BUCKETS = 32
REL_MAX_DIST = 128

D_FF = 4 * D_MODEL
RMS_EPS = 1e-6

kernel_name = "hybrid_stickbreak_dsa_streaming_trunk"


def rms_norm(x, g):
    xf = x.astype(jnp.float32)
    y = xf * lax.rsqrt(jnp.mean(xf * xf, axis=-1, keepdims=True) + RMS_EPS)
    return (y * g.astype(jnp.float32)).astype(x.dtype)


def t5_bucket(rel):
    half = REL_BUCKETS // 2
    max_exact = half // 2
    n = jnp.abs(rel)
    nf = jnp.maximum(n, max_exact).astype(jnp.float32)
    large = max_exact + (jnp.log(nf / max_exact) / math.log(REL_MAX_DIST / max_exact)
                         * (half - max_exact)).astype(jnp.int32)
    large = jnp.minimum(large, half - 1)
    return jnp.where(rel > 0, half, 0) + jnp.where(n < max_exact, n, large)


def stick_breaking_mixer(h, w_in, w_out):
    b, s, _ = h.shape
    q, k, v = jnp.split(h @ w_in, 3, axis=-1)
    q = q.reshape(b, s, SB_HEADS, SB_HEAD_DIM)
    k = k.reshape(b, s, SB_HEADS, SB_HEAD_DIM)
    v = v.reshape(b, s, SB_HEADS, SB_HEAD_DIM)
    scale = SB_HEAD_DIM ** -0.5
    outs = []
    for i in range(s // Q_BLOCK):
        q0 = i * Q_BLOCK
        q1 = q0 + Q_BLOCK
        z = jnp.einsum('bqhd,bkhd->bhqk', q[:, q0:q1], k[:, :q1],
                       preferred_element_type=jnp.float32) * scale
        t_pos = q0 + jnp.arange(Q_BLOCK)[:, None]
        s_pos = jnp.arange(q1)[None, :]
        causal = s_pos < t_pos
        log_1mb = jnp.where(causal, jax.nn.log_sigmoid(-z), 0.0)
        tail = lax.cumsum(log_1mb, axis=3, reverse=True) - log_1mb
        a = jnp.where(causal, jnp.exp(jax.nn.log_sigmoid(z) + tail), 0.0)
        outs.append(jnp.einsum('bhqk,bkhd->bqhd', a.astype(v.dtype), v[:, :q1]))
    o = jnp.concatenate(outs, axis=1).reshape(b, s, SB_HEADS * SB_HEAD_DIM)
    return o @ w_out


def dsa_mixer(h, w_in, w_out, q_gain, k_gain, rel_bias):
    b, s, _ = h.shape
    topk = min(IDX_TOPK_MAX, s // 4)
    splits = np.cumsum(DSA_SIZES)[:-1].tolist()
    q, k, v, qi, ki, wi = jnp.split(h @ w_in, splits, axis=-1)
    q = rms_norm(q.reshape(b, s, DSA_HEADS, DSA_HEAD_DIM), q_gain)
    k = rms_norm(k.reshape(b, s, DSA_KV_HEADS, DSA_HEAD_DIM), k_gain)
    v = v.reshape(b, s, DSA_KV_HEADS, DSA_HEAD_DIM)
    qi = qi.reshape(b, s, IDX_HEADS, IDX_HEAD_DIM)
    wi = wi.astype(jnp.float32) * IDX_HEADS ** -0.5
    idx_scale = IDX_HEAD_DIM ** -0.5
    att_scale = DSA_HEAD_DIM ** -0.5
    gather = jax.vmap(lambda arr, ind: arr[ind])
    outs = []
    for i in range(s // Q_BLOCK):
        q0 = i * Q_BLOCK
        q1 = q0 + Q_BLOCK
        t_pos = q0 + jnp.arange(Q_BLOCK)
        chunk_end = (t_pos // CHUNK + 1) * CHUNK
        admiss = jnp.arange(q1)[None, :] < chunk_end[:, None]
        dots = jnp.einsum('bqhd,bkd->bqhk', qi[:, q0:q1], ki[:, :q1],
                          preferred_element_type=jnp.float32) * idx_scale
        score = jnp.einsum('bqh,bqhk->bqk', wi[:, q0:q1], jax.nn.relu(dots))
        score = jnp.where(admiss[None], score, -jnp.inf)
        kb = min(topk, q1)
        _, sel = lax.top_k(score, kb)
        valid = sel < chunk_end[None, :, None]
        k_sel = gather(k[:, :q1], sel)
        v_sel = gather(v[:, :q1], sel)
        qg = q[:, q0:q1].reshape(b, Q_BLOCK, DSA_KV_HEADS, DSA_GROUP, DSA_HEAD_DIM)
        logits = jnp.einsum('bqngd,bqjnd->bqngj', qg, k_sel,
                            preferred_element_type=jnp.float32) * att_scale
        bias = rel_bias[:, t5_bucket(sel - t_pos[None, :, None])]
        bias = bias.reshape(DSA_KV_HEADS, DSA_GROUP, b, Q_BLOCK, kb).transpose(2, 3, 0, 1, 4)
        logits = jnp.where(valid[:, :, None, None, :], logits + bias.astype(jnp.float32), -jnp.inf)
        p = jax.nn.softmax(logits, axis=-1)
        o = jnp.einsum('bqngj,bqjnd->bqngd', p.astype(v.dtype), v_sel)
        outs.append(o.reshape(b, Q_BLOCK, DSA_HEADS * DSA_HEAD_DIM))
    o = jnp.concatenate(outs, axis=1)
    return o @ w_out


def setup_inputs(seed: int = 0) -> dict:
    key = jax.random.key(seed)
    ks = jax.random.split(key, 13)
    f32 = jnp.float32
    d_in = D_MODEL ** -0.5
    return {
        "x": jax.random.normal(ks[0], (BATCH, SEQ, D_MODEL), f32),
        "norm_mix": 1.0 + 0.02 * jax.random.normal(ks[1], (DEPTH, D_MODEL), f32),
        "w_in_a": jax.random.normal(ks[2], (N_A, D_MODEL, SB_IN), f32) * d_in,
        "w_out_a": jax.random.normal(ks[3], (N_A, SB_HEADS * SB_HEAD_DIM, D_MODEL), f32) * (SB_HEADS * SB_HEAD_DIM) ** -0.5,
        "w_in_b": jax.random.normal(ks[4], (N_B, D_MODEL, DSA_IN), f32) * d_in,
        "w_out_b": jax.random.normal(ks[5], (N_B, DSA_HEADS * DSA_HEAD_DIM, D_MODEL), f32) * (DSA_HEADS * DSA_HEAD_DIM) ** -0.5,
        "q_norm_b": 1.0 + 0.02 * jax.random.normal(ks[6], (N_B, DSA_HEAD_DIM), f32),
        "k_norm_b": 1.0 + 0.02 * jax.random.normal(ks[7], (N_B, DSA_HEAD_DIM), f32),
        "rel_bias": 0.5 * jax.random.normal(ks[8], (DSA_HEADS, REL_BUCKETS), f32),
        "norm_mlp": 1.0 + 0.02 * jax.random.normal(ks[9], (DEPTH, D_MODEL), f32),
        "w_up": jax.random.normal(ks[10], (DEPTH, D_MODEL, D_FF), f32) * d_in,
        "w_down": jax.random.normal(ks[11], (DEPTH, D_FF, D_MODEL), f32) * D_FF ** -0.5,
    }


def reference(x, norm_mix, w_in_a, w_out_a, w_in_b, w_out_b, q_norm_b, k_norm_b,
              rel_bias, norm_mlp, w_up, w_down):
    for i in range(DEPTH):
        h = rms_norm(x, norm_mix[i])
        j = i // N_MIXERS
        if i % N_MIXERS == 0:
            x = x + stick_breaking_mixer(h, w_in_a[j], w_out_a[j])
        else:
            x = x + dsa_mixer(h, w_in_b[j], w_out_b[j], q_norm_b[j], k_norm_b[j], rel_bias)
        h = rms_norm(x, norm_mlp[i])
        x = x + jnp.square(jax.nn.relu(h @ w_up[i])) @ w_down[i]
    return x
```

```python
import math
from contextlib import ExitStack
import numpy as np
import concourse.bass as bass
import concourse.mybir as mybir
from concourse.bass_utils import run_bass_kernel_spmd

F32 = mybir.dt.float32
BF16 = mybir.dt.bfloat16
AF = mybir.ActivationFunctionType
ALU = mybir.AluOpType

D = 2048
KT = 16
FF = 8192
NH = 16
DH = 128
NKV = 4
NEG = -30000.0
EPS = 1e-6
SB_IN = 6144
DSA_IN = 4176
TOPK = 256
SHIFT = 256.0


class Res:
    __slots__ = ("name", "reads", "writes", "pend")

    def __init__(self, name):
        self.name = name
        self.reads = {}
        self.writes = {}
        self.pend = None


def _merge(dst, src):
    for k, v in src.items():
        if dst.get(k, 0) < v:
            dst[k] = v


class Prog:
    ENGS = ("pe", "act", "dve", "pool", "sp")

    def __init__(self, nc):
        self.nc = nc
        self.ops = {e: [] for e in self.ENGS}
        self.waited = {e: {} for e in self.ENGS}
        self.count = {}
        self.pending = {e: [] for e in self.ENGS}
        self.nres = 0

    def res(self, name=None):
        self.nres += 1
        return Res((name or "r") + str(self.nres))

    def _waits(self, eng, waits):
        w = self.waited[eng]
        for k, v in waits.items():
            if w.get(k, 0) < v:
                self.ops[eng].append(("wait", k, v))
                w[k] = v

    def op(self, eng, fn, reads=(), writes=(), pwrites=(), signal=True):
        waits = {}
        for r in reads:
            assert r.pend is None or r.pend == eng, (r.name, r.pend, eng)
            _merge(waits, r.writes)
        for r in writes:
            assert r.pend is None or r.pend == eng, (r.name, r.pend, eng)
            _merge(waits, r.writes)
            _merge(waits, r.reads)
        for r in pwrites:
            assert r.pend is None or r.pend == eng, (r.name, r.pend, eng)
            _merge(waits, r.reads)
        self._waits(eng, waits)
        key = "E:" + eng
        if not signal:
            self.ops[eng].append(("op", fn, None, 0))
            self.pending[eng].append((reads, writes, pwrites))
            for r in list(reads) + list(writes) + list(pwrites):
                r.pend = eng
            return
        v = self.count.get(key, 0) + 1
        self.count[key] = v
        self.ops[eng].append(("op", fn, key, 1))
        todo = self.pending[eng] + [(reads, writes, pwrites)]
        self.pending[eng] = []
        for (rd, wr, pw) in todo:
            for r in rd:
                r.pend = None
                if r.reads.get(key, 0) < v:
                    r.reads[key] = v
            for r in wr:
                r.pend = None
                r.writes[key] = v
            for r in pw:
                r.pend = None
                r.writes[key] = v

    def dma(self, queue, fn, sb, src, dst, nowaw=False):
        for r in (src, dst):
            assert r.pend is None, (r.name, r.pend)
        waits = {}
        _merge(waits, src.writes)
        if not nowaw:
            _merge(waits, dst.writes)
        _merge(waits, dst.reads)
        self._waits(queue, waits)
        key = "D:" + sb.name
        v = self.count.get(key, 0) + 16
        self.count[key] = v
        self.ops[queue].append(("op", fn, key, 16))
        if src.reads.get(key, 0) < v:
            src.reads[key] = v
        dst.writes[key] = v

    def barrier(self):
        for e in self.ENGS:
            assert not self.pending[e]
        allc = dict(self.count)
        for e in self.ENGS:
            self._waits(e, allc)

    def emit(self):
        nc = self.nc
        with ExitStack() as st:
            sems = {}
            for k in self.count:
                sems[k] = st.enter_context(nc.semaphore(k.replace(":", "_")))
            block = st.enter_context(nc.Block())

            def replay(name):
                def f(e):
                    for it in self.ops[name]:
                        if it[0] == "wait":
                            e.wait_ge(sems[it[1]], it[2])
                        elif it[2] is None:
                            it[1](e)
                        else:
                            it[1](e).then_inc(sems[it[2]], it[3])
                return f

            block.tensor(replay("pe"))
            block.scalar(replay("act"))
            block.vector(replay("dve"))
            block.gpsimd(replay("pool"))
            block.sync(replay("sp"))


def _t5_bucket_np(rel):
    half = 16
    max_exact = 8
    n = np.abs(rel)
    nf = np.maximum(n, max_exact).astype(np.float32)
    val = (np.log(nf / np.float32(max_exact)) / np.float32(math.log(128 / max_exact))
           * np.float32(half - max_exact))
    large = max_exact + val.astype(np.int32)
    large = np.minimum(large, half - 1)
    return np.where(rel > 0, half, 0) + np.where(n < max_exact, n, large)


def _bucket_onehot():
    w = np.arange(384)
    rel = 127 - w
    b = _t5_bucket_np(rel)
    e = np.zeros((32, 384), np.float32)
    e[b, w] = 1.0
    return e


class Builder:
    def __init__(self, S, layers, dbg=False):
        self.S = S
        self.dbg = dbg
        self.layers = layers
        self.NG = S // 512
        self.NB = S // 128
        self.VS = min(1024, S)
        self.NVS = S // self.VS
        self.GV = self.VS // 512
        self.TOPK = min(TOPK, S // 4)
        nc = bass.Bass("TRN2", target_bir_lowering=False)
        self.nc = nc
        self.P = Prog(nc)
        self.st = ExitStack()
        self.bank_i = 0
        self.stage_i = 0
        self.bcnt = {}
        self.ev_i = 0

    def dram_in(self, name, shape, dt=F32):
        return self.nc.dram_tensor(name, list(shape), dt, kind="ExternalInput")

    def dram_scr(self, name, shape, dt):
        return self.nc.dram_tensor(name, list(shape), dt, kind="ExternalOutput" if self.dbg else "Internal")

    def sb(self, name, shape, dt):
        return self.st.enter_context(self.nc.sbuf_tensor(name, list(shape), dt))

    def bank(self):
        i = self.bank_i % 8
        self.bank_i += 1
        return self.pb[i], self.pb_r[i]

    def bank_from(self, lst, key):
        c = self.bcnt.get(key, 0)
        self.bcnt[key] = c + 1
        i = lst[c % len(lst)]
        return self.pb[i], self.pb_r[i]

    def evac(self, out, in_, reads, writes=(), pwrites=(), scale=None):
        self.ev_i += 1
        if self.ev_i % 2 == 0:
            self.act(out, in_, AF.Copy, reads, writes, pwrites, scale=(1.0 if scale is None else scale))
        elif scale is None:
            self.dve(lambda e: e.tensor_copy(out=out, in_=in_), reads, writes, pwrites)
        else:
            self.dve(lambda e: e.tensor_scalar(out=out, in0=in_, scalar1=float(scale), scalar2=None, op0=ALU.mult),
                     reads, writes, pwrites)

    def mm(self, out, lhsT, rhs, start, stop, reads, writes=(), pwrites=(), signal=True):
        self.P.op("pe", lambda e: e.matmul(out, lhsT=lhsT, rhs=rhs, start=start, stop=stop),
                  reads, writes, pwrites, signal)

    def act(self, out, in_, func, reads, writes=(), pwrites=(), scale=1.0, bias=0.0, accum=None):
        if accum is None:
            fn = lambda e: e.activation(out=out, in_=in_, func=func, scale=scale, bias=bias)
        else:
            fn = lambda e: e.activation(out=out, in_=in_, func=func, scale=scale, bias=bias, accum_out=accum)
        self.P.op("act", fn, reads, writes, pwrites)

    def dve(self, fn, reads, writes=(), pwrites=()):
        self.P.op("dve", fn, reads, writes, pwrites)

    def pool(self, fn, reads, writes=(), pwrites=()):
        self.P.op("pool", fn, reads, writes, pwrites)

    def load(self, out, in_, sbres, dres, queue="sp"):
        self.P.dma(queue, lambda e: e.dma_start(out=out, in_=in_), sbres, dres, sbres)

    def store(self, out, in_, sbres, dres, queue="sp"):
        self.P.dma(queue, lambda e: e.dma_start(out=out, in_=in_), sbres, sbres, dres)

    def wchunk(self, wap, r0, c0, ncols, dres, dst_c0=0, slot=None):
        if slot is None:
            i = self.wr_i % len(self.wr)
            self.wr_i += 1
        else:
            i = slot
        t, r = self.wr[i], self.wr_r[i]
        src = wap[r0:r0 + 2048, c0:c0 + ncols].rearrange("(k p) c -> p k c", p=128)
        dst = t[:, :, dst_c0:dst_c0 + ncols]
        self.P.dma("pool", lambda e: e.dma_start(out=dst, in_=src), r, dres, r)
        return t, r, i

    def setup(self):
        nc, P, S = self.nc, self.P, self.S
        self.x = self.dram_in("x", [S, D]).ap()
        self.norm_mix = self.dram_in("norm_mix", [4, D]).ap()
        self.w_in_a, self.w_out_a, self.w_in_b, self.w_out_b, self.w_up, self.w_down = {}, {}, {}, {}, {}, {}
        for (kind, mj, li) in self.layers:
            if kind == "A":
                self.w_in_a[mj] = self.dram_in(f"w_in_a{mj}", [D, SB_IN]).ap()
                self.w_out_a[mj] = self.dram_in(f"w_out_a{mj}", [D, D]).ap()
            elif kind == "B":
                self.w_in_b[mj] = self.dram_in(f"w_in_b{mj}", [D, DSA_IN]).ap()
                self.w_out_b[mj] = self.dram_in(f"w_out_b{mj}", [D, D]).ap()
            self.w_up[li] = self.dram_in(f"w_up{li}", [D, FF]).ap()
            self.w_down[li] = self.dram_in(f"w_down{li}", [FF, D]).ap()
        self.q_norm_b = self.dram_in("q_norm_b", [2, 128]).ap()
        self.k_norm_b = self.dram_in("k_norm_b", [2, 128]).ap()
        self.rel_bias = self.dram_in("rel_bias", [16, 32]).ap()
        self.norm_mlp = self.dram_in("norm_mlp", [4, D]).ap()
        self.onehot = self.dram_in("onehot", [32, 384]).ap()
        self.out = self.nc.dram_tensor("out", [S, D], F32, kind="ExternalOutput").ap()
        self.d_in = P.res("d_in")
        self.d_out = P.res("d_out")
        self.XT = self.dram_scr("XT", [KT, 128, S], F32).ap()
        self.d_XT = P.res("d_XT")
        self.QT = self.dram_scr("QT", [NH, 128, S], BF16).ap()
        self.KTt = self.dram_scr("KTt", [NH, 128, S], BF16).ap()
        self.Vh = self.dram_scr("Vh", [NH, 128, S // 128, 128], BF16).ap()
        self.QiT = self.dram_scr("QiT", [8, 128, S], BF16).ap()
        self.KiT = self.dram_scr("KiT", [128, S], BF16).ap()
        self.WI = self.dram_scr("WI", [S, 16], F32).ap()
        self.TB = self.dram_scr("TB", [16, 384], F32)
        self.d_qkv = P.res("d_qkv")
        if self.dbg:
            self.OTd = self.dram_scr("OTd", [128, NH, S], BF16).ap()
            self.SCd = self.dram_scr("SCd", [S, S], F32).ap()
            self.MKd = self.dram_scr("MKd", [S, S], BF16).ap()
        self.d_tb = P.res("d_tb")

        self.pb, self.pb_r = [], []
        for i in range(8):
            self.pb.append(self.st.enter_context(nc.psum_tensor(f"pb{i}", [128, 512], F32)))
            self.pb_r.append(P.res(f"pb{i}_"))
        self.ident = self.sb("ident", [128, 128], F32)
        self.identb = self.sb("identb", [128, 128], BF16)
        self.onesb = self.sb("onesb", [128, 128], BF16)
        self.negU = self.sb("negU", [128, 128], BF16)
        self.negones = self.sb("negones", [128, 128], BF16)
        self.maskneg = self.sb("maskneg", [128, 4, 512], BF16)
        self.adm = self.sb("adm", [128, 128], F32)
        self.gT = self.sb("gT", [128, 128], F32)
        self.qkg = self.sb("qkg", [128, 4], F32)
        self.BT = self.sb("BT", [128, 2, 16, 128], BF16)
        self.c_r = P.res("consts")
        self.bt_r = P.res("BT")
        self.SCR = self.sb("SCR", [128, 32768], BF16)
        tmpf = self.SCR[:, 0:4096].bitcast(F32)
        tmp_r = P.res("tmpf")
        c_r = self.c_r

        def aff(out, in_, pattern, cmp, fill, base, cm):
            self.pool(lambda e: e.affine_select(out=out, in_=in_, pattern=pattern, compare_op=cmp,
                                                fill=fill, base=base, channel_multiplier=cm),
                      [tmp_r], [tmp_r])

        def memset(eng, ap, val, res):
            P.op(eng, lambda e: e.memset(ap, val), [], [res])

        memset("pool", tmpf[:, 0:128], 0.0, tmp_r)
        aff(tmpf[:, 0:128], tmpf[:, 0:128], [[-1, 128]], ALU.not_equal, 1.0, 0, 1)
        self.dve(lambda e: e.tensor_copy(out=self.ident[:], in_=tmpf[:, 0:128]), [tmp_r], pwrites=[c_r])
        self.dve(lambda e: e.tensor_copy(out=self.identb[:], in_=tmpf[:, 0:128]), [tmp_r], pwrites=[c_r])
        memset("pool", tmpf[:, 128:256], -1.0, tmp_r)
        self.dve(lambda e: e.tensor_copy(out=self.negones[:], in_=tmpf[:, 128:256]), [tmp_r], pwrites=[c_r])
        aff(tmpf[:, 128:256], tmpf[:, 128:256], [[-1, 128]], ALU.is_ge, 0.0, 0, 1)
        self.dve(lambda e: e.tensor_copy(out=self.negU[:], in_=tmpf[:, 128:256]), [tmp_r], pwrites=[c_r])
        memset("pool", tmpf[:, 256:384], 1.0, tmp_r)
        self.dve(lambda e: e.tensor_copy(out=self.onesb[:], in_=tmpf[:, 256:384]), [tmp_r], pwrites=[c_r])
        for r in range(4):
            memset("pool", tmpf[:, 0:512], 0.0, tmp_r)
            aff(tmpf[:, 0:512], tmpf[:, 0:512], [[1, 512]], ALU.is_gt, NEG, -128 * r, -1)
            self.dve((lambda r: (lambda e: e.tensor_copy(out=self.maskneg[:, r, :], in_=tmpf[:, 0:512])))(r),
                     [tmp_r], pwrites=[c_r])
        memset("pool", self.adm[:], 1.0, c_r)
        memset("pool", self.adm[0:64, 64:128], 0.0, c_r)

        gsrc = tmpf[:, 512:640]
        self.load(gsrc[0:64, :], self.norm_mix.rearrange("l (k p) -> (l k) p", p=128), tmp_r, self.d_in)
        self.load(gsrc[64:128, :], self.norm_mlp.rearrange("l (k p) -> (l k) p", p=128), tmp_r, self.d_in)
        b, br = self.bank()
        P.op("pe", lambda e: e.transpose(b[:, 0:128], gsrc, self.ident[:]), [tmp_r, c_r], [br])
        self.dve(lambda e: e.tensor_copy(out=self.gT[:], in_=b[:, 0:128]), [br], pwrites=[c_r])
        g2 = tmpf[0:4, 640:768]
        self.load(g2[0:2, :], self.q_norm_b, tmp_r, self.d_in)
        self.load(g2[2:4, :], self.k_norm_b, tmp_r, self.d_in)
        b, br = self.bank()
        P.op("pe", lambda e: e.transpose(b[:, 0:4], g2, self.ident[0:4, 0:4]), [tmp_r, c_r], [br])
        self.dve(lambda e: e.tensor_copy(out=self.qkg[:], in_=b[:, 0:4]), [br], pwrites=[c_r])
        self.dve(lambda e: e.tensor_scalar(out=self.qkg[:, 0:2], in0=self.qkg[:, 0:2], scalar1=float(DH ** -0.5),
                                           scalar2=None, op0=ALU.mult), [c_r], [c_r])
        rb = tmpf[0:16, 768:800]
        self.load(rb, self.rel_bias, tmp_r, self.d_in)
        oh = tmpf[0:32, 1024:1408]
        self.load(oh, self.onehot, tmp_r, self.d_in)
        b, br = self.bank()
        P.op("pe", lambda e: e.transpose(b[0:32, 0:16], rb, self.ident[0:16, 0:16]), [tmp_r, c_r], [br])
        rbT = tmpf[0:32, 800:816]
        self.dve(lambda e: e.tensor_copy(out=rbT, in_=b[0:32, 0:16]), [br], [tmp_r])
        b2, b2r = self.bank()
        P.op("pe", lambda e: e.matmul(b2[0:16, 0:384], lhsT=rbT, rhs=oh, start=True, stop=True), [tmp_r], [b2r])
        tbs = tmpf[0:16, 1408:1792]
        self.dve(lambda e: e.tensor_copy(out=tbs, in_=b2[0:16, 0:384]), [b2r], [tmp_r])
        self.dve(lambda e: e.tensor_scalar(out=tbs, in0=tbs, scalar1=tmpf[0:16, 1408 + 382:1408 + 383], scalar2=None,
                                           op0=ALU.subtract), [tmp_r], [tmp_r])
        self.store(self.TB.ap(), tbs, tmp_r, self.d_tb)
        for p in range(128):
            for which in range(2):
                src = bass.AP(self.TB, 127 + 128 * which - p, [[0, 1], [384, 16], [1, 128]])
                dst = self.BT[p:p + 1, which, :, :]
                P.dma("pool", (lambda dst, src: (lambda e: e.dma_start(out=dst, in_=src)))(dst, src),
                      self.bt_r, self.d_tb, self.bt_r, nowaw=True)

        self.HT = self.sb("HT", [128, KT, self.VS], BF16)
        self.ht_r = [P.res(f"ht{g}_") for g in range(self.GV)]
        self.XG = self.sb("XG", [128, KT, 512], F32)
        self.xg_r = [P.res(f"xg{k}_") for k in range(KT)]
        self.wr = [self.sb(f"wr{i}", [128, KT, 512], BF16) for i in range(3)]
        self.wr_r = [P.res(f"wr{i}_") for i in range(3)]
        self.wr_i = 0
        self.stg = [self.sb(f"stg{i}", [128, 512], BF16) for i in range(4)]
        self.stg_r = [P.res(f"stg{i}_") for i in range(4)]
        self.tmpf, self.tmp_r = tmpf, tmp_r
        self.xst = [self.SCR[:, i * 4096:(i + 1) * 4096].bitcast(F32) for i in range(3)]
        self.xst_r = [tmp_r, P.res("xst1_"), P.res("xst2_")]
        self.ft = [self.sb(f"ft{i}", [128, 512], F32) for i in range(4)]
        self.ft_r = [P.res(f"ft{i}_") for i in range(4)]
        self.ft_i = 0
        self.pa_res = ([P.res("kT") for _ in range(2)], [P.res("vv") for _ in range(2)], [P.res("qq") for _ in range(2)])
        self.pb_res = (P.res("Kn"), P.res("Ki"), P.res("Vb"), [P.res("Qn") for _ in range(2)],
                       [P.res("Qi") for _ in range(2)], [P.res("wi") for _ in range(2)])
        self.wi_t = [self.sb(f"wi{i}", [128, 16], F32) for i in range(2)]
        self.m8 = self.sb("m8", [128, 8], F32)
        self.bis = self.sb("bis", [128, 4], F32)

    def ftmp(self):
        i = self.ft_i % 4
        self.ft_i += 1
        return self.ft[i], self.ft_r[i]

    def stage(self):
        i = self.stage_i % 4
        self.stage_i += 1
        return self.stg[i], self.stg_r[i]

    def prologue(self):
        P = self.P
        for g in range(self.NG):
            for tt in range(4):
                tok0 = g * 512 + tt * 128
                xin, xin_r = self.xst[(g * 4 + tt) % 3], self.xst_r[(g * 4 + tt) % 3]
                self.load(xin[:], self.x[tok0:tok0 + 128, :], xin_r, self.d_in)
                for k4 in range(4):
                    b, br = self.bank()
                    for kk in range(4):
                        k = k4 * 4 + kk
                        P.op("pe", (lambda b, kk, k, xin: (lambda e: e.transpose(b[:, kk * 128:(kk + 1) * 128],
                                                                                xin[:, k * 128:(k + 1) * 128], self.ident[:])))(b, kk, k, xin),
                             [xin_r, self.c_r], writes=[br] if kk == 0 else [], pwrites=[] if kk == 0 else [br],
                             signal=(kk == 3))
                    dst = self.XG[:, k4 * 4:(k4 + 1) * 4, tt * 128:(tt + 1) * 128]
                    src = b[:].rearrange("p (k t) -> p k t", k=4)
                    rs = [self.xg_r[k4 * 4 + kk] for kk in range(4)]
                    if k4 % 2 == 0:
                        self.dve((lambda dst, src: (lambda e: e.tensor_copy(out=dst, in_=src)))(dst, src), [br], pwrites=rs)
                    else:
                        self.act(dst, src, AF.Copy, [br], pwrites=rs)
            self.store_xg(g)

    def load_xg(self, g):
        for k in range(KT):
            self.load(self.XG[:, k, :], self.XT[k, :, g * 512:(g + 1) * 512], self.xg_r[k], self.d_XT)

    def store_xg(self, g):
        for k in range(KT):
            self.store(self.XT[k, :, g * 512:(g + 1) * 512], self.XG[:, k, :], self.xg_r[k], self.d_XT)

    def epilogue(self):
        P = self.P
        for g in range(self.NG):
            self.load_xg(g)
            for tt in range(4):
                tok0 = g * 512 + tt * 128
                xo, xo_r = self.xst[(g * 4 + tt) % 3], self.xst_r[(g * 4 + tt) % 3]
                for k4 in range(4):
                    b, br = self.bank()
                    for kk in range(4):
                        k = k4 * 4 + kk
                        P.op("pe", (lambda b, kk, k, tt: (lambda e: e.transpose(b[:, kk * 128:(kk + 1) * 128],
                                                                               self.XG[:, k, tt * 128:(tt + 1) * 128], self.ident[:])))(b, kk, k, tt),
                             [self.xg_r[k], self.c_r], writes=[br] if kk == 0 else [], pwrites=[] if kk == 0 else [br],
                             signal=(kk == 3))
                    dst = xo[:, k4 * 512:(k4 + 1) * 512]
                    if k4 % 2 == 0:
                        self.dve((lambda dst, b: (lambda e: e.tensor_copy(out=dst, in_=b[:])))(dst, b), [br],
                                 writes=[xo_r] if k4 == 0 else [], pwrites=[] if k4 == 0 else [xo_r])
                    else:
                        self.act(dst, b[:], AF.Copy, [br], pwrites=[xo_r])
                self.store(self.out[tok0:tok0 + 128, :], xo[:], xo_r, self.d_out)

    def norm_group(self, gcol, dst, dst_res, sq, sq_r):
        P = self.P
        self.act(sq[:], self.XG[:], AF.Square, self.xg_r, [sq_r])
        b, br = self.bank()
        for k in range(KT):
            self.mm(b[:], self.onesb[:], sq[:, k, :], k == 0, k == KT - 1, [sq_r, self.c_r],
                    writes=[br] if k == 0 else [], pwrites=[] if k == 0 else [br], signal=(k == KT - 1))
        lnv, lnv_r = self.ftmp()
        self.act(lnv[:], b[:], AF.Ln, [br], [lnv_r], scale=1.0 / D, bias=self.eps_ap())
        rstd, rstd_r = self.ftmp()
        self.act(rstd[:], lnv[:], AF.Exp, [lnv_r], [rstd_r], scale=-0.5)
        for k in range(KT):
            self.dve((lambda k: (lambda e: e.scalar_tensor_tensor(
                out=dst[:, k, :], in0=self.XG[:, k, :], scalar=self.gT[:, gcol + k:gcol + k + 1],
                in1=rstd[:], op0=ALU.mult, op1=ALU.mult)))(k),
                [self.xg_r[k], rstd_r, self.c_r], pwrites=[dst_res])

    def eps_ap(self):
        return self.epsb[:, 0:1]

    def post_phase(self, li, mixer_w_out, groups=None):
        P = self.P
        AT = self.SCR[:, 0:16384].rearrange("p (k t) -> p k t", k=32)
        SQ = self.SCR[:, 16384:24576].rearrange("p (k t) -> p k t", k=16)
        HG = self.SCR[:, 24576:32768].rearrange("p (k t) -> p k t", k=16)
        at_r = [P.res("at") for _ in range(32)]
        sq_r = P.res("sq")
        hg_r = P.res("hg")
        for g in (groups if groups is not None else range(self.NG)):
            vs, gl = divmod(g, self.GV)
            self.load_xg(g)
            if mixer_w_out is not None:
                for oc in range(4):
                    w, wres, _ = self.wchunk(mixer_w_out, 0, oc * 512, 512, self.d_in)
                    for ft in range(4):
                        b, br = self.bank()
                        for k in range(KT):
                            self.mm(b[:], w[:, k, ft * 128:(ft + 1) * 128], self.HT[:, k, gl * 512:(gl + 1) * 512],
                                    k == 0, k == KT - 1, [wres, self.ht_r[gl]],
                                    writes=[br] if k == 0 else [], pwrites=[] if k == 0 else [br], signal=(k == KT - 1))
                        kx = oc * 4 + ft
                        self.dve((lambda kx, b: (lambda e: e.tensor_tensor(out=self.XG[:, kx, :], in0=self.XG[:, kx, :],
                                                                          in1=b[:], op=ALU.add)))(kx, b),
                                 [br, self.xg_r[kx]], [self.xg_r[kx]])
            self.norm_group(64 + li * 16, HG, hg_r, SQ, sq_r)
            for half in range(2):
                for uc in range(8):
                    c0 = (half * 8 + uc) * 512
                    w, wres, _ = self.wchunk(self.w_up[li], 0, c0, 512, self.d_in)
                    for ft in range(4):
                        b, br = self.bank()
                        for k in range(KT):
                            self.mm(b[:], w[:, k, ft * 128:(ft + 1) * 128], HG[:, k, :], k == 0, k == KT - 1,
                                    [wres, hg_r], writes=[br] if k == 0 else [], pwrites=[] if k == 0 else [br],
                                    signal=(k == KT - 1))
                        fi = uc * 4 + ft
                        tf, tfr = self.ftmp()
                        self.act(tf[:], b[:], AF.Relu, [br], [tfr])
                        self.dve((lambda fi, tf: (lambda e: e.tensor_tensor(out=AT[:, fi, :], in0=tf[:], in1=tf[:],
                                                                           op=ALU.mult)))(fi, tf),
                                 [tfr], [at_r[fi]])
                for oc in range(4):
                    banks = [self.bank() for _ in range(4)]
                    for kq in range(2):
                        r0 = half * 4096 + kq * 2048
                        w, wres, _ = self.wchunk(self.w_down[li], r0, oc * 512, 512, self.d_in)
                        for ft in range(4):
                            b, br = banks[ft]
                            for k in range(KT):
                                first = (kq == 0 and k == 0)
                                last = (kq == 1 and k == KT - 1)
                                self.mm(b[:], w[:, k, ft * 128:(ft + 1) * 128], AT[:, kq * 16 + k, :], first, last,
                                        [wres, at_r[kq * 16 + k]], writes=[br] if first else [],
                                        pwrites=[] if first else [br], signal=(k == KT - 1))
                    for ft in range(4):
                        b, br = banks[ft]
                        kx = oc * 4 + ft
                        self.dve((lambda kx, b: (lambda e: e.tensor_tensor(out=self.XG[:, kx, :], in0=self.XG[:, kx, :],
                                                                          in1=b[:], op=ALU.add)))(kx, b),
                                 [br, self.xg_r[kx]], [self.xg_r[kx]])
            self.store_xg(g)

    def p1_norm(self, li, vs):
        P = self.P
        SQ = self.SCR[:, 16384:24576].rearrange("p (k t) -> p k t", k=16)
        sq_r = P.res("sq")
        for gl in range(self.GV):
            self.load_xg(vs * self.GV + gl)
            self.norm_group(li * 16, self.HT[:, :, gl * 512:(gl + 1) * 512], self.ht_r[gl], SQ, sq_r)

    def proj_fm(self, w, wres, gl, ft):
        b, br = self.bank()
        for k in range(KT):
            self.mm(b[:], w[:, k, ft * 128:(ft + 1) * 128], self.HT[:, k, gl * 512:(gl + 1) * 512],
                    k == 0, k == KT - 1, [wres, self.ht_r[gl]],
                    writes=[br] if k == 0 else [], pwrites=[] if k == 0 else [br], signal=(k == KT - 1))
        return b, br

    def proj_tm(self, w, wres, tt, c0, ncols):
        b, br = self.bank()
        for k in range(KT):
            self.mm(b[:, 0:ncols], self.HT[:, k, tt * 128:(tt + 1) * 128], w[:, k, c0:c0 + ncols],
                    k == 0, k == KT - 1, [wres, self.ht_r[tt // 4]],
                    writes=[br] if k == 0 else [], pwrites=[] if k == 0 else [br], signal=(k == KT - 1))
        return b, br

    def p1_A(self, li, mj, vs):
        VS, GV = self.VS, self.GV
        self.p1_norm(li, vs)
        for c in range(12):
            w, wres, _ = self.wchunk(self.w_in_a[mj], 0, c * 512, 512, self.d_in)
            if c < 8:
                dstT = self.QT if c < 4 else self.KTt
                scale = float(DH ** -0.5) if c < 4 else None
                for gl in range(GV):
                    tok0 = vs * VS + gl * 512
                    for ft in range(4):
                        h = (c % 4) * 4 + ft
                        b, br = self.proj_fm(w, wres, gl, ft)
                        sg, sgr = self.stage()
                        self.evac(sg[:], b[:], [br], [sgr], scale=scale)
                        self.store(dstT[h, :, tok0:tok0 + 512], sg[:], sgr, self.d_qkv)
            else:
                cv = c - 8
                for tt in range(VS // 128):
                    b, br = self.proj_tm(w, wres, tt, 0, 512)
                    sg, sgr = self.stage()
                    self.evac(sg[:], b[:], [br], [sgr])
                    blk = (vs * VS) // 128 + tt
                    self.store(self.Vh[cv * 4:(cv + 1) * 4, :, blk, :].rearrange("h p d -> p h d"),
                               sg[:].rearrange("p (h d) -> p h d", h=4), sgr, self.d_qkv)

    def p2_A(self, vs):
        P = self.P
        S, VS, GV = self.S, self.VS, self.GV
        nk = (vs + 1) * VS
        o = 0
        kTs, vvs, qqs = [], [], []
        for pp in range(2):
            kTs.append(self.SCR[:, o:o + S]); o += S
        for pp in range(2):
            vvs.append(self.SCR[:, o:o + S].rearrange("p (b d) -> p b d", d=128)); o += S
        for pp in range(2):
            qqs.append(self.SCR[:, o:o + VS]); o += VS
        Ls = [self.SCR[:, o + i * 512:o + (i + 1) * 512] for i in range(3)]; o += 1536
        As = [self.SCR[:, o + i * 512:o + (i + 1) * 512] for i in range(3)]; o += 1536
        Lsums = [self.SCR[:, o + i * 512:o + (i + 1) * 512] for i in range(2)]; o += 1024
        assert o <= 32768
        k_r, v_r, q_r = self.pa_res
        L_r = [P.res("L") for _ in range(3)]
        A_r = [P.res("A") for _ in range(3)]
        Ls_r = [P.res("Lsum") for _ in range(2)]
        steps = []
        for h in range(NH):
            for gl in range(GV):
                gg = vs * GV + gl
                for kb in range(4 * gg + 3, -1, -1):
                    steps.append(dict(h=h, gl=gl, gg=gg, kb=kb, first=(kb == 4 * gg + 3), last=(kb == 0), i=len(steps)))
        st = {"lsi": 0, "O": None}

        def load_head(h):
            pp = h % 2
            self.load(kTs[pp][:, 0:nk], self.KTt[h, :, 0:nk], k_r[pp], self.d_qkv)
            self.load(vvs[pp][:, 0:nk // 128, :], self.Vh[h, :, 0:nk // 128, :], v_r[pp], self.d_qkv)
            self.load(qqs[pp][:, 0:VS], self.QT[h, :, vs * VS:(vs + 1) * VS], q_r[pp], self.d_qkv)

        def front(s):
            pp = s["h"] % 2
            kb, gg, gl = s["kb"], s["gg"], s["gl"]
            diag = kb >= 4 * gg
            r = kb - 4 * gg
            ks = kTs[pp][:, kb * 128:(kb + 1) * 128]
            qs = qqs[pp][:, gl * 512:(gl + 1) * 512]
            s["ks"], s["qs"], s["diag"], s["r"], s["pp"] = ks, qs, diag, r, pp
            li_ = s["i"] % 3
            s["L"], s["Lr"] = Ls[li_], L_r[li_]
            s["A"], s["Ar"] = As[li_], A_r[li_]
            Z, Zr = self.bank_from([0, 1, 2], "Z")
            self.mm(Z[:], ks, qs, True, not diag, [k_r[pp], q_r[pp]], writes=[Zr], signal=not diag)
            if diag:
                self.mm(Z[:], self.identb[:], self.maskneg[:, r, :], False, True, [self.c_r], pwrites=[Zr])
            E, Er = self.ftmp()
            self.act(E[:], Z[:], AF.Exp, [Zr], [Er])
            self.act(s["L"], E[:], AF.Ln, [Er], [s["Lr"]], bias=self.one_ap())

        def mid(s):
            pp, ks, qs, diag, r, first = s["pp"], s["ks"], s["qs"], s["diag"], s["r"], s["first"]
            L, Lr, A, Ar = s["L"], s["Lr"], s["A"], s["Ar"]
            T, Tr = self.bank_from([3, 4, 5], "T")
            self.mm(T[:], ks, qs, True, False, [k_r[pp], q_r[pp]], writes=[Tr], signal=False)
            if diag:
                self.mm(T[:], self.identb[:], self.maskneg[:, r, :], False, False, [self.c_r], pwrites=[Tr],
                        signal=False)
            self.mm(T[:], self.negU[:], L, False, first, [Lr, self.c_r], pwrites=[Tr], signal=first)
            lsi = st["lsi"]
            if not first:
                self.mm(T[:], self.negones[:], Lsums[lsi], False, True, [Ls_r[lsi], self.c_r], pwrites=[Tr])
            self.act(A, T[:], AF.Exp, [Tr], [Ar])
            if not s["last"]:
                if first:
                    st["lsi"] = 0
                    self.dve((lambda L, dst: (lambda e: e.tensor_copy(out=dst, in_=L)))(L, Lsums[0]), [Lr], [Ls_r[0]])
                else:
                    nsi = 1 - lsi
                    self.dve((lambda L, a, dst: (lambda e: e.tensor_tensor(out=dst, in0=a, in1=L, op=ALU.add)))(
                        L, Lsums[lsi], Lsums[nsi]), [Lr, Ls_r[lsi]], [Ls_r[nsi]])
                    st["lsi"] = nsi

        def back(s):
            pp, first = s["pp"], s["first"]
            if first:
                st["O"] = self.bank_from([6, 7], "O")
            O, Or = st["O"]
            self.mm(O[:], vvs[pp][:, s["kb"], :], s["A"], first, s["last"], [v_r[pp], s["Ar"]],
                    writes=[Or] if first else [], pwrites=[] if first else [Or])
            if s["last"]:
                gl, h = s["gl"], s["h"]
                self.evac(self.HT[:, h, gl * 512:(gl + 1) * 512], O[:], [Or], pwrites=[self.ht_r[gl]])

        n = len(steps)
        load_head(0)
        front(steps[0])
        for k in range(n):
            if k + 1 < n:
                front(steps[k + 1])
            mid(steps[k])
            if k >= 1:
                back(steps[k - 1])
            sk = steps[k]
            if sk["first"] and sk["gl"] == 0 and sk["h"] + 1 < NH:
                load_head(sk["h"] + 1)
        back(steps[n - 1])

    def one_ap(self):
        return self.epsb[:, 1:2]

    def p1_B(self, li, mj, vs):
        P = self.P
        VS, GV = self.VS, self.GV
        W = self.w_in_b[mj]
        self.p1_norm(li, vs)
        for c in range(8):
            w, wres, _ = self.wchunk(W, 0, c * 512, 512, self.d_in)
            if c < 5:
                isq = c < 4
                gcol = mj if isq else 2 + mj
                for gl in range(GV):
                    tok0 = vs * VS + gl * 512
                    for ft in range(4):
                        b, br = self.proj_fm(w, wres, gl, ft)
                        sq, sqr = self.stage()
                        self.act(sq[:], b[:], AF.Square, [br], [sqr])
                        b2, b2r = self.bank()
                        self.mm(b2[:], self.onesb[:], sq[:], True, True, [sqr, self.c_r], writes=[b2r])
                        ln, lnr = self.ftmp()
                        self.act(ln[:], b2[:], AF.Ln, [b2r], [lnr], scale=1.0 / DH, bias=self.eps_ap())
                        rs, rsr = self.ftmp()
                        self.act(rs[:], ln[:], AF.Exp, [lnr], [rsr], scale=-0.5)
                        sg, sgr = self.stage()
                        self.dve((lambda sg, b, rs, gcol: (lambda e: e.scalar_tensor_tensor(
                            out=sg[:], in0=b[:], scalar=self.qkg[:, gcol:gcol + 1], in1=rs[:],
                            op0=ALU.mult, op1=ALU.mult)))(sg, b, rs, gcol), [br, rsr, self.c_r], [sgr])
                        if isq:
                            dst = self.QT[c * 4 + ft, :, tok0:tok0 + 512]
                        else:
                            dst = self.KTt[ft, :, tok0:tok0 + 512]
                        self.store(dst, sg[:], sgr, self.d_qkv)
            elif c == 5:
                for tt in range(VS // 128):
                    b, br = self.proj_tm(w, wres, tt, 0, 512)
                    sg, sgr = self.stage()
                    self.evac(sg[:], b[:], [br], [sgr])
                    blk = (vs * VS) // 128 + tt
                    self.store(self.Vh[0:4, :, blk, :].rearrange("h p d -> p h d"),
                               sg[:].rearrange("p (h d) -> p h d", h=4), sgr, self.d_qkv)
            else:
                for gl in range(GV):
                    tok0 = vs * VS + gl * 512
                    for ft in range(4):
                        b, br = self.proj_fm(w, wres, gl, ft)
                        sg, sgr = self.stage()
                        self.evac(sg[:], b[:], [br], [sgr], scale=0.125)
                        self.store(self.QiT[(c - 6) * 4 + ft, :, tok0:tok0 + 512], sg[:], sgr, self.d_qkv)
        w, wres, si = self.wchunk(W, 0, 4096, 64, self.d_in, dst_c0=0)
        self.wchunk(W, 0, 4096, 64, self.d_in, dst_c0=64, slot=si)
        self.wchunk(W, 0, 4160, 16, self.d_in, dst_c0=128, slot=si)
        for gl in range(GV):
            tok0 = vs * VS + gl * 512
            b, br = self.proj_fm(w, wres, gl, 0)
            sg, sgr = self.stage()
            self.evac(sg[:], b[:], [br], [sgr])
            self.store(self.KiT[:, tok0:tok0 + 512], sg[:], sgr, self.d_qkv)
        for tt in range(VS // 128):
            b, br = self.proj_tm(w, wres, tt, 128, 16)
            f, fr = self.ftmp()
            self.evac(f[:, 0:16], b[:, 0:16], [br], [fr], scale=0.25)
            tok0 = vs * VS + tt * 128
            self.store(self.WI[tok0:tok0 + 128, :], f[:, 0:16], fr, self.d_qkv)

    def p2_B(self, vs):
        P = self.P
        S, VS = self.S, self.VS
        nk = (vs + 1) * VS
        nkb_all = nk // 128
        o = 0
        Kn = self.SCR[:, 0:4 * S].rearrange("p (n s) -> p n s", n=4); o = 4 * S
        Ki = self.SCR[:, o:o + S]; o += S
        Qn = [self.SCR[:, o + i * 2048:o + (i + 1) * 2048].rearrange("p (h t) -> p h t", h=16) for i in range(2)]; o += 4096
        Qi = [self.SCR[:, o + i * 1024:o + (i + 1) * 1024].rearrange("p (j t) -> p j t", j=8) for i in range(2)]; o += 2048
        PT = [self.SCR[:, o + i * 512:o + (i + 1) * 512] for i in range(3)]; o += 1536
        assert o <= 32768
        xgf = self.XG[:].rearrange("p k t -> p (k t)")
        score = xgf[:, 0:4096]
        work = xgf[:, 4096:8192]
        w0 = self.wr[0][:].rearrange("p k t -> p (k t)")
        maskb = w0[:, 0:4096]
        maskTs = [w0[:, 4096:8192].rearrange("p (b t) -> p b t", t=128),
                  self.SCR[:, o:o + 4096].rearrange("p (b t) -> p b t", t=128)]
        assert o + 4096 <= 32768
        Vb = []
        for n in range(4):
            wf = self.wr[1 + n // 2][:].rearrange("p k t -> p (k t)")
            Vb.append(wf[:, (n % 2) * 4096:(n % 2 + 1) * 4096].rearrange("p (b d) -> p b d", d=128))
        kn_r, ki_r, vb_r, qn_r, qi_r, wi_r = self.pb_res
        pt_r = [P.res("PT") for _ in range(3)]
        sc_r = [P.res("score") for _ in range(8)]
        work_r, m8_r, maskb_r = P.res("work"), P.res("m8"), P.res("maskb")
        maskT_rs = [P.res("maskT") for _ in range(2)]
        mid_r, cnt_r, tt_r = P.res("mid"), P.res("cnt"), P.res("tt")
        for n in range(4):
            self.load(Kn[:, n, 0:nk], self.KTt[n, :, 0:nk], kn_r, self.d_qkv) if n == 0 else \
                P.dma("sp", (lambda n: (lambda e: e.dma_start(out=Kn[:, n, 0:nk], in_=self.KTt[n, :, 0:nk])))(n),
                      kn_r, self.d_qkv, kn_r, nowaw=True)
            P.dma("sp", (lambda n: (lambda e: e.dma_start(out=Vb[n][:, 0:nkb_all, :], in_=self.Vh[n, :, 0:nkb_all, :])))(n),
                  vb_r, self.d_qkv, vb_r, nowaw=(n > 0))
        self.load(Ki[:, 0:nk], self.KiT[:, 0:nk], ki_r, self.d_qkv)
        pst = {"pti": 0}

        def front(tl):
            i = (vs * VS) // 128 + tl
            t0 = i * 128
            q1 = t0 + 128
            nkb = i + 1
            pp = tl % 2
            self.load(Qn[pp], self.QT[:, :, t0:t0 + 128].rearrange("h p t -> p h t"), qn_r[pp], self.d_qkv)
            self.load(Qi[pp], self.QiT[:, :, t0:t0 + 128].rearrange("j p t -> p j t"), qi_r[pp], self.d_qkv)
            wi = self.wi_t[pp]
            self.load(wi[:], self.WI[t0:t0 + 128, :], wi_r[pp], self.d_qkv)
            nch = (q1 + 511) // 512
            for c in range(nch):
                c0 = c * 512
                wc = min(512, q1 - c0)
                for ih in range(16):
                    j, half = divmod(ih, 2)
                    Db, Dr = self.bank_from([0, 1, 2, 3], "D")
                    self.mm(Db[:, 0:wc], Qi[pp][half * 64:(half + 1) * 64, j, :], Ki[half * 64:(half + 1) * 64, c0:c0 + wc],
                            True, True, [qi_r[pp], ki_r], writes=[Dr])
                    R, Rr = self.ftmp()
                    self.act(R[:, 0:wc], Db[:, 0:wc], AF.Relu, [Dr], [Rr])
                    sc = score[:, c0:c0 + wc]
                    if ih == 0:
                        self.dve((lambda sc, R, wc, wi: (lambda e: e.tensor_scalar(
                            out=sc, in0=R[:, 0:wc], scalar1=wi[:, 0:1], scalar2=SHIFT, op0=ALU.mult, op1=ALU.add)))(sc, R, wc, wi),
                            [Rr, wi_r[pp]], [sc_r[c]])
                    else:
                        self.dve((lambda sc, R, wc, wi, ih: (lambda e: e.scalar_tensor_tensor(
                            out=sc, in0=R[:, 0:wc], scalar=wi[:, ih:ih + 1], in1=sc, op0=ALU.mult, op1=ALU.add)))(sc, R, wc, wi, ih),
                            [Rr, wi_r[pp], sc_r[c]], [sc_r[c]])
            cd = t0 // 512
            self.dve((lambda t0: (lambda e: e.tensor_tensor(out=score[:, t0:t0 + 128], in0=score[:, t0:t0 + 128],
                                                           in1=self.adm[:], op=ALU.mult)))(t0),
                     [sc_r[cd], self.c_r], [sc_r[cd]])
            if self.dbg:
                for c in range(nch):
                    c0 = c * 512; wc = min(512, q1 - c0)
                    self.store(self.SCd[t0:t0 + 128, c0:c0 + wc], score[:, c0:c0 + wc], sc_r[c], self.d_qkv)
            m8 = self.m8
            TK = self.TOPK
            if q1 <= TK:
                P.op("dve", lambda e: e.memset(m8[:, 7:8], 1.0), [], [m8_r])
            else:
                mid, cnt, tt = self.bis[:, 0:1], self.bis[:, 1:2], self.bis[:, 2:3]
                P.op("dve", lambda e: e.memset(mid, SHIFT), [], [mid_r])
                NR = 21
                for r in range(NR):
                    w_r = 64.0 * 2.0 ** (-r)
                    w_n = 64.0 * 2.0 ** (-(r + 1))
                    self.dve(lambda e: e.tensor_scalar(out=work[:, 0:q1], in0=score[:, 0:q1], scalar1=mid, scalar2=0.0,
                                                       op0=ALU.is_ge, op1=ALU.add, accum_out=cnt),
                             sc_r[0:nch] + [mid_r], [work_r, cnt_r])
                    self.dve((lambda w_r: (lambda e: e.tensor_scalar(out=tt, in0=cnt, scalar1=float(TK), scalar2=w_r,
                                                                     op0=ALU.is_ge, op1=ALU.mult)))(w_r), [cnt_r], [tt_r])
                    self.dve((lambda w_n: (lambda e: e.scalar_tensor_tensor(out=mid, in0=tt, scalar=-w_n, in1=mid,
                                                                            op0=ALU.add, op1=ALU.add)))(w_n),
                             [tt_r, mid_r], [mid_r])
                w_R = 64.0 * 2.0 ** (-NR)
                self.dve(lambda e: e.tensor_scalar(out=m8[:, 7:8], in0=mid, scalar1=-w_R, scalar2=None, op0=ALU.add),
                         [mid_r], [m8_r])
            self.dve(lambda e: e.tensor_scalar(out=maskb[:, 0:q1], in0=score[:, 0:q1], scalar1=m8[:, 7:8], scalar2=NEG,
                                               op0=ALU.is_lt, op1=ALU.mult), sc_r[0:nch] + [m8_r], [maskb_r])
            if self.dbg:
                self.store(self.MKd[t0:t0 + 128, 0:q1], maskb[:, 0:q1], maskb_r, self.d_qkv)
        def trans(tl):
            i = (vs * VS) // 128 + tl
            t0 = i * 128
            q1 = t0 + 128
            nkb = i + 1
            pp = tl % 2
            mT = maskTs[tl % 2]
            mT_r = maskT_rs[tl % 2]
            for kb0 in range(0, nkb, 8):
                n8 = min(8, nkb - kb0)
                tb, tbr = self.bank_from([0, 1, 2, 3], "D")
                tbv = tb[:].bitcast(BF16)
                for kk in range(n8):
                    kb = kb0 + kk
                    P.op("pe", (lambda tbv, kk, kb: (lambda e: e.transpose(tbv[:, kk * 128:(kk + 1) * 128],
                                                                          maskb[:, kb * 128:(kb + 1) * 128], self.identb[:])))(tbv, kk, kb),
                         [maskb_r, self.c_r], writes=[tbr] if kk == 0 else [], pwrites=[] if kk == 0 else [tbr],
                         signal=(kk == n8 - 1))
                dst = mT[:, kb0:kb0 + n8, :].rearrange("p b t -> p (b t)")
                self.evac(dst, tbv[:, 0:n8 * 128], [tbr], writes=[mT_r] if kb0 == 0 else [],
                          pwrites=[] if kb0 == 0 else [mT_r])
        def back(tl):
            i = (vs * VS) // 128 + tl
            t0 = i * 128
            q1 = t0 + 128
            nkb = i + 1
            pp = tl % 2
            mT = maskTs[tl % 2]
            mT_r = maskT_rs[tl % 2]
            for n in range(4):
                O, Or = self.bank_from([6, 7], "O")
                Dn, Dnr = self.bank_from([4, 5], "Dn")
                qs = Qn[pp][:, 4 * n:4 * n + 4, :]
                for kb in range(nkb):
                    ST, STr = self.bank_from([0, 1, 2, 3], "D")
                    near = kb >= i - 1
                    self.mm(ST[:], Kn[:, n, kb * 128:(kb + 1) * 128], qs, True, False, [kn_r, qn_r[pp]], writes=[STr],
                            signal=False)
                    for hh in range(4):
                        lastm = (hh == 3) and not near
                        self.mm(ST[:, hh * 128:(hh + 1) * 128], self.identb[:], mT[:, kb, :], False, lastm,
                                [mT_r, self.c_r], pwrites=[STr], signal=lastm)
                    if near:
                        which = 0 if kb == i else 1
                        self.mm(ST[:], self.identb[:], self.BT[:, which, 4 * n:4 * n + 4, :], False, True,
                                [self.bt_r, self.c_r], pwrites=[STr])
                    pt, ptr = PT[pst["pti"] % 3], pt_r[pst["pti"] % 3]
                    pst["pti"] += 1
                    self.act(pt, ST[:], AF.Exp, [STr], [ptr])
                    f0 = kb == 0
                    l0 = kb == nkb - 1
                    self.mm(O[:], Vb[n][:, kb, :], pt, f0, l0, [vb_r, ptr], writes=[Or] if f0 else [],
                            pwrites=[] if f0 else [Or], signal=False)
                    self.mm(Dn[:], self.onesb[:], pt, f0, l0, [ptr, self.c_r], writes=[Dnr] if f0 else [],
                            pwrites=[] if f0 else [Dnr])
                self.act(Dn[:], Dn[:], AF.Ln, [Dnr], [Dnr])
                rec, recr = self.ftmp()
                self.act(rec[:], Dn[:], AF.Exp, [Dnr], [recr], scale=-1.0)
                oc, ocr = self.ftmp()
                self.act(oc[:], O[:], AF.Copy, [Or], [ocr])
                dst = self.HT[:, 4 * n:4 * n + 4, tl * 128:(tl + 1) * 128]
                self.pool((lambda dst, oc, rec: (lambda e: e.tensor_tensor(
                    out=dst, in0=oc[:].rearrange("p (h t) -> p h t", h=4), in1=rec[:].rearrange("p (h t) -> p h t", h=4),
                    op=ALU.mult)))(dst, oc, rec), [ocr, recr], pwrites=[self.ht_r[tl // 4]])

        NT = VS // 128
        for tl in range(NT):
            front(tl)
            if tl >= 1:
                back(tl - 1)
            trans(tl)
        back(NT - 1)

    def build(self):
        self.setup()
        P = self.P
        self.epsb = self.sb("epsb", [128, 2], F32)
        P.op("pool", lambda e: e.memset(self.epsb[:, 0:1], EPS), [], [self.c_r])
        P.op("pool", lambda e: e.memset(self.epsb[:, 1:2], 1.0), [], [self.c_r])
        self.prologue()
        P.barrier()
        for (kind, mj, li) in self.layers:
            if kind is None:
                self.post_phase(li, None)
                P.barrier()
                continue
            for vs in range(self.NVS):
                groups = list(range(vs * self.GV, (vs + 1) * self.GV))
                if kind == "A":
                    self.p1_A(li, mj, vs)
                    P.barrier()
                    self.p2_A(vs)
                    P.barrier()
                    self.post_phase(li, self.w_out_a[mj], groups)
                else:
                    self.p1_B(li, mj, vs)
                    P.barrier()
                    self.p2_B(vs)
                    P.barrier()
                    if self.dbg:
                        self.store(self.OTd[:, :, vs * self.VS:(vs + 1) * self.VS], self.HT[:], self.ht_r[0], self.d_qkv)
                        P.barrier()
                    self.post_phase(li, self.w_out_b[mj], groups)
                P.barrier()
        self.epilogue()
        P.barrier()
        self.ninstr = {k: len(v) for k, v in P.ops.items()}
        P.emit()
        self.st.close()
        return self.nc


def make_inputs_common(inputs, layers):
    f = lambda a: np.ascontiguousarray(np.asarray(a, dtype=np.float32))
    common = {k: f(inputs[k]) for k in ("norm_mix", "q_norm_b", "k_norm_b", "rel_bias", "norm_mlp")}
    for (kind, mj, li) in layers:
        if kind == "A":
            common[f"w_in_a{mj}"] = f(inputs["w_in_a"][mj])
            common[f"w_out_a{mj}"] = f(inputs["w_out_a"][mj])
        elif kind == "B":
            common[f"w_in_b{mj}"] = f(inputs["w_in_b"][mj])
            common[f"w_out_b{mj}"] = f(inputs["w_out_b"][mj])
        common[f"w_up{li}"] = f(inputs["w_up"][li])
        common[f"w_down{li}"] = f(inputs["w_down"][li])
    common["onehot"] = _bucket_onehot()
    return common


def kernel(**inputs):
    x = np.asarray(inputs["x"], dtype=np.float32)
    B, S, _ = x.shape
    layers = [("A", 0, 0), ("B", 0, 1), ("A", 1, 2), ("B", 1, 3)]
    nc = Builder(S, layers).build()
    common = make_inputs_common(inputs, layers)
    in_maps = []
    for c in range(B):
        m = dict(common)
        m["x"] = np.ascontiguousarray(x[c])
        in_maps.append(m)
    res = run_bass_kernel_spmd(nc, in_maps, core_ids=list(range(B)))
    out = np.stack([res.results[b]["out"] for b in range(B)], axis=0)
    return out.astype(np.float32)
```

```python
import math
from contextlib import ExitStack
import numpy as np
import concourse.bass as bass
import concourse.mybir as mybir
from concourse.bass_utils import run_bass_kernel_spmd

F32 = mybir.dt.float32
BF16 = mybir.dt.bfloat16
AF = mybir.ActivationFunctionType
ALU = mybir.AluOpType

D = 2048
KT = 16
FF = 8192
NH = 16
DH = 128
NKV = 4
NEG = -30000.0
EPS = 1e-6
SB_IN = 6144
DSA_IN = 4176
TOPK = 256
SHIFT = 256.0


class Res:
    __slots__ = ("name", "reads", "writes", "pend")

    def __init__(self, name):
        self.name = name
        self.reads = {}
        self.writes = {}
        self.pend = None


def _merge(dst, src):
    for k, v in src.items():
        if dst.get(k, 0) < v:
            dst[k] = v


class Prog:
    ENGS = ("pe", "act", "dve", "pool", "sp")

    def __init__(self, nc):
        self.nc = nc
        self.ops = {e: [] for e in self.ENGS}
        self.waited = {e: {} for e in self.ENGS}
        self.count = {}
        self.pending = {e: [] for e in self.ENGS}
        self.nres = 0

    def res(self, name=None):
        self.nres += 1
        return Res((name or "r") + str(self.nres))

    def _waits(self, eng, waits):
        w = self.waited[eng]
        for k, v in waits.items():
            if w.get(k, 0) < v:
                self.ops[eng].append(("wait", k, v))
                w[k] = v

    def op(self, eng, fn, reads=(), writes=(), pwrites=(), signal=True):
        waits = {}
        for r in reads:
            assert r.pend is None or r.pend == eng, (r.name, r.pend, eng)
            _merge(waits, r.writes)
        for r in writes:
            assert r.pend is None or r.pend == eng, (r.name, r.pend, eng)
            _merge(waits, r.writes)
            _merge(waits, r.reads)
        for r in pwrites:
            assert r.pend is None or r.pend == eng, (r.name, r.pend, eng)
            _merge(waits, r.reads)
        self._waits(eng, waits)
        key = "E:" + eng
        if not signal:
            self.ops[eng].append(("op", fn, None, 0))
            self.pending[eng].append((reads, writes, pwrites))
            for r in list(reads) + list(writes) + list(pwrites):
                r.pend = eng
            return
        v = self.count.get(key, 0) + 1
        self.count[key] = v
        self.ops[eng].append(("op", fn, key, 1))
        todo = self.pending[eng] + [(reads, writes, pwrites)]
        self.pending[eng] = []
        for (rd, wr, pw) in todo:
            for r in rd:
                r.pend = None
                if r.reads.get(key, 0) < v:
                    r.reads[key] = v
            for r in wr:
                r.pend = None
                r.writes[key] = v
            for r in pw:
                r.pend = None
                r.writes[key] = v

    def dma(self, queue, fn, sb, src, dst, nowaw=False):
        for r in (src, dst):
            assert r.pend is None, (r.name, r.pend)
        waits = {}
        _merge(waits, src.writes)
        if not nowaw:
            _merge(waits, dst.writes)
        _merge(waits, dst.reads)
        self._waits(queue, waits)
        key = "D:" + sb.name
        v = self.count.get(key, 0) + 16
        self.count[key] = v
        self.ops[queue].append(("op", fn, key, 16))
        if src.reads.get(key, 0) < v:
            src.reads[key] = v
        dst.writes[key] = v

    def barrier(self):
        for e in self.ENGS:
            assert not self.pending[e]
        allc = dict(self.count)
        for e in self.ENGS:
            self._waits(e, allc)

    def emit(self):
        nc = self.nc
        with ExitStack() as st:
            sems = {}
            for k in self.count:
                sems[k] = st.enter_context(nc.semaphore(k.replace(":", "_")))
            block = st.enter_context(nc.Block())

            def replay(name):
                def f(e):
                    for it in self.ops[name]:
                        if it[0] == "wait":
                            e.wait_ge(sems[it[1]], it[2])
                        elif it[2] is None:
                            it[1](e)
                        else:
                            it[1](e).then_inc(sems[it[2]], it[3])
                return f

            block.tensor(replay("pe"))
            block.scalar(replay("act"))
            block.vector(replay("dve"))
            block.gpsimd(replay("pool"))
            block.sync(replay("sp"))


def _t5_bucket_np(rel):
    half = 16
    max_exact = 8
    n = np.abs(rel)
    nf = np.maximum(n, max_exact).astype(np.float32)
    val = (np.log(nf / np.float32(max_exact)) / np.float32(math.log(128 / max_exact))
           * np.float32(half - max_exact))
    large = max_exact + val.astype(np.int32)
    large = np.minimum(large, half - 1)
    return np.where(rel > 0, half, 0) + np.where(n < max_exact, n, large)


def _bucket_onehot():
    w = np.arange(384)
    rel = 127 - w
    b = _t5_bucket_np(rel)
    e = np.zeros((32, 384), np.float32)
    e[b, w] = 1.0
    return e


class Builder:
    def __init__(self, S, layers, dbg=False):
        self.S = S
        self.dbg = dbg
        self.layers = layers
        self.NG = S // 512
        self.NB = S // 128
        self.VS = min(1024, S)
        self.NVS = S // self.VS
        self.GV = self.VS // 512
        self.TOPK = min(TOPK, S // 4)
        nc = bass.Bass("TRN2", target_bir_lowering=False)
        self.nc = nc
        self.P = Prog(nc)
        self.st = ExitStack()
        self.bank_i = 0
        self.stage_i = 0
        self.bcnt = {}
        self.ev_i = 0

    def dram_in(self, name, shape, dt=F32):
        return self.nc.dram_tensor(name, list(shape), dt, kind="ExternalInput")

    def dram_scr(self, name, shape, dt):
        return self.nc.dram_tensor(name, list(shape), dt, kind="ExternalOutput" if self.dbg else "Internal")

    def sb(self, name, shape, dt):
        return self.st.enter_context(self.nc.sbuf_tensor(name, list(shape), dt))

    def bank(self):
        i = self.bank_i % 8
        self.bank_i += 1
        return self.pb[i], self.pb_r[i]

    def bank_from(self, lst, key):
        c = self.bcnt.get(key, 0)
        self.bcnt[key] = c + 1
        i = lst[c % len(lst)]
        return self.pb[i], self.pb_r[i]

    def evac(self, out, in_, reads, writes=(), pwrites=(), scale=None):
        self.ev_i += 1
        if self.ev_i % 2 == 0:
            self.act(out, in_, AF.Copy, reads, writes, pwrites, scale=(1.0 if scale is None else scale))
        elif scale is None:
            self.dve(lambda e: e.tensor_copy(out=out, in_=in_), reads, writes, pwrites)
        else:
            self.dve(lambda e: e.tensor_scalar(out=out, in0=in_, scalar1=float(scale), scalar2=None, op0=ALU.mult),
                     reads, writes, pwrites)

    def mm(self, out, lhsT, rhs, start, stop, reads, writes=(), pwrites=(), signal=True):
        self.P.op("pe", lambda e: e.matmul(out, lhsT=lhsT, rhs=rhs, start=start, stop=stop),
                  reads, writes, pwrites, signal)

    def act(self, out, in_, func, reads, writes=(), pwrites=(), scale=1.0, bias=0.0, accum=None):
        if accum is None:
            fn = lambda e: e.activation(out=out, in_=in_, func=func, scale=scale, bias=bias)
        else:
            fn = lambda e: e.activation(out=out, in_=in_, func=func, scale=scale, bias=bias, accum_out=accum)
        self.P.op("act", fn, reads, writes, pwrites)

    def dve(self, fn, reads, writes=(), pwrites=()):
        self.P.op("dve", fn, reads, writes, pwrites)

    def pool(self, fn, reads, writes=(), pwrites=()):
        self.P.op("pool", fn, reads, writes, pwrites)

    def load(self, out, in_, sbres, dres, queue="sp"):
        self.P.dma(queue, lambda e: e.dma_start(out=out, in_=in_), sbres, dres, sbres)

    def store(self, out, in_, sbres, dres, queue="sp"):
        self.P.dma(queue, lambda e: e.dma_start(out=out, in_=in_), sbres, sbres, dres)

    def wchunk(self, wap, r0, c0, ncols, dres, dst_c0=0, slot=None):
        if slot is None:
            i = self.wr_i % len(self.wr)
            self.wr_i += 1
        else:
            i = slot
        t, r = self.wr[i], self.wr_r[i]
        src = wap[r0:r0 + 2048, c0:c0 + ncols].rearrange("(k p) c -> p k c", p=128)
        dst = t[:, :, dst_c0:dst_c0 + ncols]
        self.P.dma("pool", lambda e: e.dma_start(out=dst, in_=src), r, dres, r)
        return t, r, i

    def setup(self):
        nc, P, S = self.nc, self.P, self.S
        self.x = self.dram_in("x", [S, D]).ap()
        self.norm_mix = self.dram_in("norm_mix", [4, D]).ap()
        self.w_in_a, self.w_out_a, self.w_in_b, self.w_out_b, self.w_up, self.w_down = {}, {}, {}, {}, {}, {}
        for (kind, mj, li) in self.layers:
            if kind == "A":
                self.w_in_a[mj] = self.dram_in(f"w_in_a{mj}", [D, SB_IN]).ap()
                self.w_out_a[mj] = self.dram_in(f"w_out_a{mj}", [D, D]).ap()
            elif kind == "B":
                self.w_in_b[mj] = self.dram_in(f"w_in_b{mj}", [D, DSA_IN]).ap()
                self.w_out_b[mj] = self.dram_in(f"w_out_b{mj}", [D, D]).ap()
            self.w_up[li] = self.dram_in(f"w_up{li}", [D, FF]).ap()
            self.w_down[li] = self.dram_in(f"w_down{li}", [FF, D]).ap()
        self.q_norm_b = self.dram_in("q_norm_b", [2, 128]).ap()
        self.k_norm_b = self.dram_in("k_norm_b", [2, 128]).ap()
        self.rel_bias = self.dram_in("rel_bias", [16, 32]).ap()
        self.norm_mlp = self.dram_in("norm_mlp", [4, D]).ap()
        self.onehot = self.dram_in("onehot", [32, 384]).ap()
        self.out = self.nc.dram_tensor("out", [S, D], F32, kind="ExternalOutput").ap()
        self.d_in = P.res("d_in")
        self.d_out = P.res("d_out")
        self.XT = self.dram_scr("XT", [KT, 128, S], F32).ap()
        self.d_XT = P.res("d_XT")
        self.QT = self.dram_scr("QT", [NH, 128, S], BF16).ap()
        self.KTt = self.dram_scr("KTt", [NH, 128, S], BF16).ap()
        self.Vh = self.dram_scr("Vh", [NH, 128, S // 128, 128], BF16).ap()
        self.QiT = self.dram_scr("QiT", [8, 128, S], BF16).ap()
        self.KiT = self.dram_scr("KiT", [128, S], BF16).ap()
        self.WI = self.dram_scr("WI", [S, 16], F32).ap()
        self.TB = self.dram_scr("TB", [16, 384], F32)
        self.d_qkv = P.res("d_qkv")
        if self.dbg:
            self.OTd = self.dram_scr("OTd", [128, NH, S], BF16).ap()
            self.SCd = self.dram_scr("SCd", [S, S], F32).ap()
            self.MKd = self.dram_scr("MKd", [S, S], BF16).ap()
        self.d_tb = P.res("d_tb")

        self.pb, self.pb_r = [], []
        for i in range(8):
            self.pb.append(self.st.enter_context(nc.psum_tensor(f"pb{i}", [128, 512], F32)))
            self.pb_r.append(P.res(f"pb{i}_"))
        self.ident = self.sb("ident", [128, 128], F32)
        self.identb = self.sb("identb", [128, 128], BF16)
        self.onesb = self.sb("onesb", [128, 128], BF16)
        self.negU = self.sb("negU", [128, 128], BF16)
        self.negones = self.sb("negones", [128, 128], BF16)
        self.maskneg = self.sb("maskneg", [128, 4, 512], BF16)
        self.adm = self.sb("adm", [128, 128], F32)
        self.gT = self.sb("gT", [128, 128], F32)
        self.qkg = self.sb("qkg", [128, 4], F32)
        self.BT = self.sb("BT", [128, 2, 16, 128], BF16)
        self.c_r = P.res("consts")
        self.bt_r = P.res("BT")
        self.SCR = self.sb("SCR", [128, 32768], BF16)
        tmpf = self.SCR[:, 0:4096].bitcast(F32)
        tmp_r = P.res("tmpf")
        c_r = self.c_r

        def aff(out, in_, pattern, cmp, fill, base, cm):
            self.pool(lambda e: e.affine_select(out=out, in_=in_, pattern=pattern, compare_op=cmp,
                                                fill=fill, base=base, channel_multiplier=cm),
                      [tmp_r], [tmp_r])

        def memset(eng, ap, val, res):
            P.op(eng, lambda e: e.memset(ap, val), [], [res])

        memset("pool", tmpf[:, 0:128], 0.0, tmp_r)
        aff(tmpf[:, 0:128], tmpf[:, 0:128], [[-1, 128]], ALU.not_equal, 1.0, 0, 1)
        self.dve(lambda e: e.tensor_copy(out=self.ident[:], in_=tmpf[:, 0:128]), [tmp_r], pwrites=[c_r])
        self.dve(lambda e: e.tensor_copy(out=self.identb[:], in_=tmpf[:, 0:128]), [tmp_r], pwrites=[c_r])
        memset("pool", tmpf[:, 128:256], -1.0, tmp_r)
        self.dve(lambda e: e.tensor_copy(out=self.negones[:], in_=tmpf[:, 128:256]), [tmp_r], pwrites=[c_r])
        aff(tmpf[:, 128:256], tmpf[:, 128:256], [[-1, 128]], ALU.is_ge, 0.0, 0, 1)
        self.dve(lambda e: e.tensor_copy(out=self.negU[:], in_=tmpf[:, 128:256]), [tmp_r], pwrites=[c_r])
        memset("pool", tmpf[:, 256:384], 1.0, tmp_r)
        self.dve(lambda e: e.tensor_copy(out=self.onesb[:], in_=tmpf[:, 256:384]), [tmp_r], pwrites=[c_r])
        for r in range(4):
            memset("pool", tmpf[:, 0:512], 0.0, tmp_r)
            aff(tmpf[:, 0:512], tmpf[:, 0:512], [[1, 512]], ALU.is_gt, NEG, -128 * r, -1)
            self.dve((lambda r: (lambda e: e.tensor_copy(out=self.maskneg[:, r, :], in_=tmpf[:, 0:512])))(r),
                     [tmp_r], pwrites=[c_r])
        memset("pool", self.adm[:], 1.0, c_r)
        memset("pool", self.adm[0:64, 64:128], 0.0, c_r)

        gsrc = tmpf[:, 512:640]
        self.load(gsrc[0:64, :], self.norm_mix.rearrange("l (k p) -> (l k) p", p=128), tmp_r, self.d_in)
        self.load(gsrc[64:128, :], self.norm_mlp.rearrange("l (k p) -> (l k) p", p=128), tmp_r, self.d_in)
        b, br = self.bank()
        P.op("pe", lambda e: e.transpose(b[:, 0:128], gsrc, self.ident[:]), [tmp_r, c_r], [br])
        self.dve(lambda e: e.tensor_copy(out=self.gT[:], in_=b[:, 0:128]), [br], pwrites=[c_r])
        g2 = tmpf[0:4, 640:768]
        self.load(g2[0:2, :], self.q_norm_b, tmp_r, self.d_in)
        self.load(g2[2:4, :], self.k_norm_b, tmp_r, self.d_in)
        b, br = self.bank()
        P.op("pe", lambda e: e.transpose(b[:, 0:4], g2, self.ident[0:4, 0:4]), [tmp_r, c_r], [br])
        self.dve(lambda e: e.tensor_copy(out=self.qkg[:], in_=b[:, 0:4]), [br], pwrites=[c_r])
        self.dve(lambda e: e.tensor_scalar(out=self.qkg[:, 0:2], in0=self.qkg[:, 0:2], scalar1=float(DH ** -0.5),
                                           scalar2=None, op0=ALU.mult), [c_r], [c_r])
        rb = tmpf[0:16, 768:800]
        self.load(rb, self.rel_bias, tmp_r, self.d_in)
        oh = tmpf[0:32, 1024:1408]
        self.load(oh, self.onehot, tmp_r, self.d_in)
        b, br = self.bank()
        P.op("pe", lambda e: e.transpose(b[0:32, 0:16], rb, self.ident[0:16, 0:16]), [tmp_r, c_r], [br])
        rbT = tmpf[0:32, 800:816]
        self.dve(lambda e: e.tensor_copy(out=rbT, in_=b[0:32, 0:16]), [br], [tmp_r])
        b2, b2r = self.bank()
        P.op("pe", lambda e: e.matmul(b2[0:16, 0:384], lhsT=rbT, rhs=oh, start=True, stop=True), [tmp_r], [b2r])
        tbs = tmpf[0:16, 1408:1792]
        self.dve(lambda e: e.tensor_copy(out=tbs, in_=b2[0:16, 0:384]), [b2r], [tmp_r])
        self.dve(lambda e: e.tensor_scalar(out=tbs, in0=tbs, scalar1=tmpf[0:16, 1408 + 382:1408 + 383], scalar2=None,
                                           op0=ALU.subtract), [tmp_r], [tmp_r])
        self.store(self.TB.ap(), tbs, tmp_r, self.d_tb)
        for p in range(128):
            for which in range(2):
                src = bass.AP(self.TB, 127 + 128 * which - p, [[0, 1], [384, 16], [1, 128]])
                dst = self.BT[p:p + 1, which, :, :]
                P.dma("pool", (lambda dst, src: (lambda e: e.dma_start(out=dst, in_=src)))(dst, src),
                      self.bt_r, self.d_tb, self.bt_r, nowaw=True)

        self.HT = self.sb("HT", [128, KT, self.VS], BF16)
        self.ht_r = [P.res(f"ht{g}_") for g in range(self.GV)]
        self.XG = self.sb("XG", [128, KT, 512], F32)
        self.xg_r = [P.res(f"xg{k}_") for k in range(KT)]
        self.wr = [self.sb(f"wr{i}", [128, KT, 512], BF16) for i in range(3)]
        self.wr_r = [P.res(f"wr{i}_") for i in range(3)]
        self.wr_i = 0
        self.stg = [self.sb(f"stg{i}", [128, 512], BF16) for i in range(4)]
        self.stg_r = [P.res(f"stg{i}_") for i in range(4)]
        self.tmpf, self.tmp_r = tmpf, tmp_r
        self.ft = [self.sb(f"ft{i}", [128, 512], F32) for i in range(4)]
        self.ft_r = [P.res(f"ft{i}_") for i in range(4)]
        self.ft_i = 0
        self.pa_res = ([P.res("kT") for _ in range(2)], [P.res("vv") for _ in range(2)], [P.res("qq") for _ in range(2)])
        self.pb_res = (P.res("Kn"), P.res("Ki"), P.res("Vb"), [P.res("Qn") for _ in range(2)],
                       [P.res("Qi") for _ in range(2)], [P.res("wi") for _ in range(2)])
        self.wi_t = [self.sb(f"wi{i}", [128, 16], F32) for i in range(2)]
        self.m8 = self.sb("m8", [128, 8], F32)
        self.bis = self.sb("bis", [128, 4], F32)

    def ftmp(self):
        i = self.ft_i % 4
        self.ft_i += 1
        return self.ft[i], self.ft_r[i]

    def stage(self):
        i = self.stage_i % 4
        self.stage_i += 1
        return self.stg[i], self.stg_r[i]

    def prologue(self):
        P = self.P
        xin = self.tmpf
        for g in range(self.NG):
            for tt in range(4):
                tok0 = g * 512 + tt * 128
                self.load(xin[:], self.x[tok0:tok0 + 128, :], self.tmp_r, self.d_in)
                for k4 in range(4):
                    b, br = self.bank()
                    for kk in range(4):
                        k = k4 * 4 + kk
                        P.op("pe", (lambda b, kk, k: (lambda e: e.transpose(b[:, kk * 128:(kk + 1) * 128],
                                                                           xin[:, k * 128:(k + 1) * 128], self.ident[:])))(b, kk, k),
                             [self.tmp_r, self.c_r], writes=[br] if kk == 0 else [], pwrites=[] if kk == 0 else [br],
                             signal=(kk == 3))
                    dst = self.XG[:, k4 * 4:(k4 + 1) * 4, tt * 128:(tt + 1) * 128]
                    src = b[:].rearrange("p (k t) -> p k t", k=4)
                    rs = [self.xg_r[k4 * 4 + kk] for kk in range(4)]
                    if k4 % 2 == 0:
                        self.dve((lambda dst, src: (lambda e: e.tensor_copy(out=dst, in_=src)))(dst, src), [br], pwrites=rs)
                    else:
                        self.act(dst, src, AF.Copy, [br], pwrites=rs)
            self.store_xg(g)

    def load_xg(self, g):
        for k in range(KT):
            self.load(self.XG[:, k, :], self.XT[k, :, g * 512:(g + 1) * 512], self.xg_r[k], self.d_XT)

    def store_xg(self, g):
        for k in range(KT):
            self.store(self.XT[k, :, g * 512:(g + 1) * 512], self.XG[:, k, :], self.xg_r[k], self.d_XT)

    def epilogue(self):
        P = self.P
        xo = self.tmpf
        for g in range(self.NG):
            self.load_xg(g)
            for tt in range(4):
                tok0 = g * 512 + tt * 128
                for k4 in range(4):
                    b, br = self.bank()
                    for kk in range(4):
                        k = k4 * 4 + kk
                        P.op("pe", (lambda b, kk, k, tt: (lambda e: e.transpose(b[:, kk * 128:(kk + 1) * 128],
                                                                               self.XG[:, k, tt * 128:(tt + 1) * 128], self.ident[:])))(b, kk, k, tt),
                             [self.xg_r[k], self.c_r], writes=[br] if kk == 0 else [], pwrites=[] if kk == 0 else [br],
                             signal=(kk == 3))
                    dst = xo[:, k4 * 512:(k4 + 1) * 512]
                    if k4 % 2 == 0:
                        self.dve((lambda dst, b: (lambda e: e.tensor_copy(out=dst, in_=b[:])))(dst, b), [br],
                                 writes=[self.tmp_r] if k4 == 0 else [], pwrites=[] if k4 == 0 else [self.tmp_r])
                    else:
                        self.act(dst, b[:], AF.Copy, [br], pwrites=[self.tmp_r])
                self.store(self.out[tok0:tok0 + 128, :], xo[:], self.tmp_r, self.d_out)

    def norm_group(self, gcol, dst, dst_res, sq, sq_r):
        P = self.P
        self.act(sq[:], self.XG[:], AF.Square, self.xg_r, [sq_r])
        b, br = self.bank()
        for k in range(KT):
            self.mm(b[:], self.onesb[:], sq[:, k, :], k == 0, k == KT - 1, [sq_r, self.c_r],
                    writes=[br] if k == 0 else [], pwrites=[] if k == 0 else [br], signal=(k == KT - 1))
        lnv, lnv_r = self.ftmp()
        self.act(lnv[:], b[:], AF.Ln, [br], [lnv_r], scale=1.0 / D, bias=self.eps_ap())
        rstd, rstd_r = self.ftmp()
        self.act(rstd[:], lnv[:], AF.Exp, [lnv_r], [rstd_r], scale=-0.5)
        for k in range(KT):
            self.dve((lambda k: (lambda e: e.scalar_tensor_tensor(
                out=dst[:, k, :], in0=self.XG[:, k, :], scalar=self.gT[:, gcol + k:gcol + k + 1],
                in1=rstd[:], op0=ALU.mult, op1=ALU.mult)))(k),
                [self.xg_r[k], rstd_r, self.c_r], pwrites=[dst_res])

    def eps_ap(self):
        return self.epsb[:, 0:1]

    def post_phase(self, li, mixer_w_out, groups=None):
        P = self.P
        AT = self.SCR[:, 0:16384].rearrange("p (k t) -> p k t", k=32)
        SQ = self.SCR[:, 16384:24576].rearrange("p (k t) -> p k t", k=16)
        HG = self.SCR[:, 24576:32768].rearrange("p (k t) -> p k t", k=16)
        at_r = [P.res("at") for _ in range(32)]
        sq_r = P.res("sq")
        hg_r = P.res("hg")
        for g in (groups if groups is not None else range(self.NG)):
            vs, gl = divmod(g, self.GV)
            self.load_xg(g)
            if mixer_w_out is not None:
                for oc in range(4):
                    w, wres, _ = self.wchunk(mixer_w_out, 0, oc * 512, 512, self.d_in)
                    for ft in range(4):
                        b, br = self.bank()
                        for k in range(KT):
                            self.mm(b[:], w[:, k, ft * 128:(ft + 1) * 128], self.HT[:, k, gl * 512:(gl + 1) * 512],
                                    k == 0, k == KT - 1, [wres, self.ht_r[gl]],
                                    writes=[br] if k == 0 else [], pwrites=[] if k == 0 else [br], signal=(k == KT - 1))
                        kx = oc * 4 + ft
                        self.dve((lambda kx, b: (lambda e: e.tensor_tensor(out=self.XG[:, kx, :], in0=self.XG[:, kx, :],
                                                                          in1=b[:], op=ALU.add)))(kx, b),
                                 [br, self.xg_r[kx]], [self.xg_r[kx]])
            self.norm_group(64 + li * 16, HG, hg_r, SQ, sq_r)
            for half in range(2):
                for uc in range(8):
                    c0 = (half * 8 + uc) * 512
                    w, wres, _ = self.wchunk(self.w_up[li], 0, c0, 512, self.d_in)
                    for ft in range(4):
                        b, br = self.bank()
                        for k in range(KT):
                            self.mm(b[:], w[:, k, ft * 128:(ft + 1) * 128], HG[:, k, :], k == 0, k == KT - 1,
                                    [wres, hg_r], writes=[br] if k == 0 else [], pwrites=[] if k == 0 else [br],
                                    signal=(k == KT - 1))
                        fi = uc * 4 + ft
                        tf, tfr = self.ftmp()
                        self.act(tf[:], b[:], AF.Relu, [br], [tfr])
                        self.dve((lambda fi, tf: (lambda e: e.tensor_tensor(out=AT[:, fi, :], in0=tf[:], in1=tf[:],
                                                                           op=ALU.mult)))(fi, tf),
                                 [tfr], [at_r[fi]])
                for oc in range(4):
                    banks = [self.bank() for _ in range(4)]
                    for kq in range(2):
                        r0 = half * 4096 + kq * 2048
                        w, wres, _ = self.wchunk(self.w_down[li], r0, oc * 512, 512, self.d_in)
                        for ft in range(4):
                            b, br = banks[ft]
                            for k in range(KT):
                                first = (kq == 0 and k == 0)
                                last = (kq == 1 and k == KT - 1)
                                self.mm(b[:], w[:, k, ft * 128:(ft + 1) * 128], AT[:, kq * 16 + k, :], first, last,
                                        [wres, at_r[kq * 16 + k]], writes=[br] if first else [],
                                        pwrites=[] if first else [br], signal=(k == KT - 1))
                    for ft in range(4):
                        b, br = banks[ft]
                        kx = oc * 4 + ft
                        self.dve((lambda kx, b: (lambda e: e.tensor_tensor(out=self.XG[:, kx, :], in0=self.XG[:, kx, :],
                                                                          in1=b[:], op=ALU.add)))(kx, b),
                                 [br, self.xg_r[kx]], [self.xg_r[kx]])
            self.store_xg(g)

    def p1_norm(self, li, vs):
        P = self.P
        SQ = self.SCR[:, 16384:24576].rearrange("p (k t) -> p k t", k=16)
        sq_r = P.res("sq")
        for gl in range(self.GV):
            self.load_xg(vs * self.GV + gl)
            self.norm_group(li * 16, self.HT[:, :, gl * 512:(gl + 1) * 512], self.ht_r[gl], SQ, sq_r)

    def proj_fm(self, w, wres, gl, ft):
        b, br = self.bank()
        for k in range(KT):
            self.mm(b[:], w[:, k, ft * 128:(ft + 1) * 128], self.HT[:, k, gl * 512:(gl + 1) * 512],
                    k == 0, k == KT - 1, [wres, self.ht_r[gl]],
                    writes=[br] if k == 0 else [], pwrites=[] if k == 0 else [br], signal=(k == KT - 1))
        return b, br

    def proj_tm(self, w, wres, tt, c0, ncols):
        b, br = self.bank()
        for k in range(KT):
            self.mm(b[:, 0:ncols], self.HT[:, k, tt * 128:(tt + 1) * 128], w[:, k, c0:c0 + ncols],
                    k == 0, k == KT - 1, [wres, self.ht_r[tt // 4]],
                    writes=[br] if k == 0 else [], pwrites=[] if k == 0 else [br], signal=(k == KT - 1))
        return b, br

    def p1_A(self, li, mj, vs):
        VS, GV = self.VS, self.GV
        self.p1_norm(li, vs)
        for c in range(12):
            w, wres, _ = self.wchunk(self.w_in_a[mj], 0, c * 512, 512, self.d_in)
            if c < 8:
                dstT = self.QT if c < 4 else self.KTt
                scale = float(DH ** -0.5) if c < 4 else None
                for gl in range(GV):
                    tok0 = vs * VS + gl * 512
                    for ft in range(4):
                        h = (c % 4) * 4 + ft
                        b, br = self.proj_fm(w, wres, gl, ft)
                        sg, sgr = self.stage()
                        self.evac(sg[:], b[:], [br], [sgr], scale=scale)
                        self.store(dstT[h, :, tok0:tok0 + 512], sg[:], sgr, self.d_qkv)
            else:
                cv = c - 8
                for tt in range(VS // 128):
                    b, br = self.proj_tm(w, wres, tt, 0, 512)
                    sg, sgr = self.stage()
                    self.evac(sg[:], b[:], [br], [sgr])
                    blk = (vs * VS) // 128 + tt
                    self.store(self.Vh[cv * 4:(cv + 1) * 4, :, blk, :].rearrange("h p d -> p h d"),
                               sg[:].rearrange("p (h d) -> p h d", h=4), sgr, self.d_qkv)

    def p2_A(self, vs):
        P = self.P
        S, VS, GV = self.S, self.VS, self.GV
        nk = (vs + 1) * VS
        o = 0
        kTs, vvs, qqs = [], [], []
        for pp in range(2):
            kTs.append(self.SCR[:, o:o + S]); o += S
        for pp in range(2):
            vvs.append(self.SCR[:, o:o + S].rearrange("p (b d) -> p b d", d=128)); o += S
        for pp in range(2):
            qqs.append(self.SCR[:, o:o + VS]); o += VS
        Ls = [self.SCR[:, o + i * 512:o + (i + 1) * 512] for i in range(3)]; o += 1536
        As = [self.SCR[:, o + i * 512:o + (i + 1) * 512] for i in range(3)]; o += 1536
        Lsums = [self.SCR[:, o + i * 512:o + (i + 1) * 512] for i in range(2)]; o += 1024
        assert o <= 32768
        k_r, v_r, q_r = self.pa_res
        L_r = [P.res("L") for _ in range(3)]
        A_r = [P.res("A") for _ in range(3)]
        Ls_r = [P.res("Lsum") for _ in range(2)]
        steps = []
        for h in range(NH):
            for gl in range(GV):
                gg = vs * GV + gl
                for kb in range(4 * gg + 3, -1, -1):
                    steps.append(dict(h=h, gl=gl, gg=gg, kb=kb, first=(kb == 4 * gg + 3), last=(kb == 0), i=len(steps)))
        st = {"lsi": 0, "O": None}

        def load_head(h):
            pp = h % 2
            self.load(kTs[pp][:, 0:nk], self.KTt[h, :, 0:nk], k_r[pp], self.d_qkv)
            self.load(vvs[pp][:, 0:nk // 128, :], self.Vh[h, :, 0:nk // 128, :], v_r[pp], self.d_qkv)
            self.load(qqs[pp][:, 0:VS], self.QT[h, :, vs * VS:(vs + 1) * VS], q_r[pp], self.d_qkv)

        def front(s):
            pp = s["h"] % 2
            kb, gg, gl = s["kb"], s["gg"], s["gl"]
            diag = kb >= 4 * gg
            r = kb - 4 * gg
            ks = kTs[pp][:, kb * 128:(kb + 1) * 128]
            qs = qqs[pp][:, gl * 512:(gl + 1) * 512]
            s["ks"], s["qs"], s["diag"], s["r"], s["pp"] = ks, qs, diag, r, pp
            li_ = s["i"] % 3
            s["L"], s["Lr"] = Ls[li_], L_r[li_]
            s["A"], s["Ar"] = As[li_], A_r[li_]
            Z, Zr = self.bank_from([0, 1, 2], "Z")
            self.mm(Z[:], ks, qs, True, not diag, [k_r[pp], q_r[pp]], writes=[Zr], signal=not diag)
            if diag:
                self.mm(Z[:], self.identb[:], self.maskneg[:, r, :], False, True, [self.c_r], pwrites=[Zr])
            E, Er = self.ftmp()
            self.act(E[:], Z[:], AF.Exp, [Zr], [Er])
            self.act(s["L"], E[:], AF.Ln, [Er], [s["Lr"]], bias=1.0)

        def mid(s):
            pp, ks, qs, diag, r, first = s["pp"], s["ks"], s["qs"], s["diag"], s["r"], s["first"]
            L, Lr, A, Ar = s["L"], s["Lr"], s["A"], s["Ar"]
            T, Tr = self.bank_from([3, 4, 5], "T")
            self.mm(T[:], ks, qs, True, False, [k_r[pp], q_r[pp]], writes=[Tr], signal=False)
            if diag:
                self.mm(T[:], self.identb[:], self.maskneg[:, r, :], False, False, [self.c_r], pwrites=[Tr],
                        signal=False)
            self.mm(T[:], self.negU[:], L, False, first, [Lr, self.c_r], pwrites=[Tr], signal=first)
            lsi = st["lsi"]
            if not first:
                self.mm(T[:], self.negones[:], Lsums[lsi], False, True, [Ls_r[lsi], self.c_r], pwrites=[Tr])
            self.act(A, T[:], AF.Exp, [Tr], [Ar])
            if not s["last"]:
                if first:
                    st["lsi"] = 0
                    self.dve((lambda L, dst: (lambda e: e.tensor_copy(out=dst, in_=L)))(L, Lsums[0]), [Lr], [Ls_r[0]])
                else:
                    nsi = 1 - lsi
                    self.dve((lambda L, a, dst: (lambda e: e.tensor_tensor(out=dst, in0=a, in1=L, op=ALU.add)))(
                        L, Lsums[lsi], Lsums[nsi]), [Lr, Ls_r[lsi]], [Ls_r[nsi]])
                    st["lsi"] = nsi

        def back(s):
            pp, first = s["pp"], s["first"]
            if first:
                st["O"] = self.bank_from([6, 7], "O")
            O, Or = st["O"]
            self.mm(O[:], vvs[pp][:, s["kb"], :], s["A"], first, s["last"], [v_r[pp], s["Ar"]],
                    writes=[Or] if first else [], pwrites=[] if first else [Or])
            if s["last"]:
                gl, h = s["gl"], s["h"]
                self.evac(self.HT[:, h, gl * 512:(gl + 1) * 512], O[:], [Or], pwrites=[self.ht_r[gl]])

        n = len(steps)
        load_head(0)
        front(steps[0])
        for k in range(n):
            if k + 1 < n:
                front(steps[k + 1])
            mid(steps[k])
            if k >= 1:
                back(steps[k - 1])
            sk = steps[k]
            if sk["first"] and sk["gl"] == 0 and sk["h"] + 1 < NH:
                load_head(sk["h"] + 1)
        back(steps[n - 1])

    def one_ap(self):
        return self.epsb[:, 1:2]

    def p1_B(self, li, mj, vs):
        P = self.P
        VS, GV = self.VS, self.GV
        W = self.w_in_b[mj]
        self.p1_norm(li, vs)
        for c in range(8):
            w, wres, _ = self.wchunk(W, 0, c * 512, 512, self.d_in)
            if c < 5:
                isq = c < 4
                gcol = mj if isq else 2 + mj
                for gl in range(GV):
                    tok0 = vs * VS + gl * 512
                    for ft in range(4):
                        b, br = self.proj_fm(w, wres, gl, ft)
                        sq, sqr = self.stage()
                        self.act(sq[:], b[:], AF.Square, [br], [sqr])
                        b2, b2r = self.bank()
                        self.mm(b2[:], self.onesb[:], sq[:], True, True, [sqr, self.c_r], writes=[b2r])
                        ln, lnr = self.ftmp()
                        self.act(ln[:], b2[:], AF.Ln, [b2r], [lnr], scale=1.0 / DH, bias=self.eps_ap())
                        rs, rsr = self.ftmp()
                        self.act(rs[:], ln[:], AF.Exp, [lnr], [rsr], scale=-0.5)
                        sg, sgr = self.stage()
                        self.dve((lambda sg, b, rs, gcol: (lambda e: e.scalar_tensor_tensor(
                            out=sg[:], in0=b[:], scalar=self.qkg[:, gcol:gcol + 1], in1=rs[:],
                            op0=ALU.mult, op1=ALU.mult)))(sg, b, rs, gcol), [br, rsr, self.c_r], [sgr])
                        if isq:
                            dst = self.QT[c * 4 + ft, :, tok0:tok0 + 512]
                        else:
                            dst = self.KTt[ft, :, tok0:tok0 + 512]
                        self.store(dst, sg[:], sgr, self.d_qkv)
            elif c == 5:
                for tt in range(VS // 128):
                    b, br = self.proj_tm(w, wres, tt, 0, 512)
                    sg, sgr = self.stage()
                    self.evac(sg[:], b[:], [br], [sgr])
                    blk = (vs * VS) // 128 + tt
                    self.store(self.Vh[0:4, :, blk, :].rearrange("h p d -> p h d"),
                               sg[:].rearrange("p (h d) -> p h d", h=4), sgr, self.d_qkv)
            else:
                for gl in range(GV):
                    tok0 = vs * VS + gl * 512
                    for ft in range(4):
                        b, br = self.proj_fm(w, wres, gl, ft)
                        sg, sgr = self.stage()
                        self.evac(sg[:], b[:], [br], [sgr], scale=0.125)
                        self.store(self.QiT[(c - 6) * 4 + ft, :, tok0:tok0 + 512], sg[:], sgr, self.d_qkv)
        w, wres, si = self.wchunk(W, 0, 4096, 64, self.d_in, dst_c0=0)
        self.wchunk(W, 0, 4096, 64, self.d_in, dst_c0=64, slot=si)
        self.wchunk(W, 0, 4160, 16, self.d_in, dst_c0=128, slot=si)
        for gl in range(GV):
            tok0 = vs * VS + gl * 512
            b, br = self.proj_fm(w, wres, gl, 0)
            sg, sgr = self.stage()
            self.evac(sg[:], b[:], [br], [sgr])
            self.store(self.KiT[:, tok0:tok0 + 512], sg[:], sgr, self.d_qkv)
        for tt in range(VS // 128):
            b, br = self.proj_tm(w, wres, tt, 128, 16)
            f, fr = self.ftmp()
            self.evac(f[:, 0:16], b[:, 0:16], [br], [fr], scale=0.25)
            tok0 = vs * VS + tt * 128
            self.store(self.WI[tok0:tok0 + 128, :], f[:, 0:16], fr, self.d_qkv)

    def p2_B(self, vs):
        P = self.P
        S, VS = self.S, self.VS
        nk = (vs + 1) * VS
        nkb_all = nk // 128
        o = 0
        Kn = self.SCR[:, 0:4 * S].rearrange("p (n s) -> p n s", n=4); o = 4 * S
        Ki = self.SCR[:, o:o + S]; o += S
        Qn = [self.SCR[:, o + i * 2048:o + (i + 1) * 2048].rearrange("p (h t) -> p h t", h=16) for i in range(2)]; o += 4096
        Qi = [self.SCR[:, o + i * 1024:o + (i + 1) * 1024].rearrange("p (j t) -> p j t", j=8) for i in range(2)]; o += 2048
        PT = [self.SCR[:, o + i * 512:o + (i + 1) * 512] for i in range(3)]; o += 1536
        assert o <= 32768
        xgf = self.XG[:].rearrange("p k t -> p (k t)")
        score = xgf[:, 0:4096]
        work = xgf[:, 4096:8192]
        w0 = self.wr[0][:].rearrange("p k t -> p (k t)")
        maskb = w0[:, 0:4096]
        maskTs = [w0[:, 4096:8192].rearrange("p (b t) -> p b t", t=128),
                  self.SCR[:, o:o + 4096].rearrange("p (b t) -> p b t", t=128)]
        assert o + 4096 <= 32768
        Vb = []
        for n in range(4):
            wf = self.wr[1 + n // 2][:].rearrange("p k t -> p (k t)")
            Vb.append(wf[:, (n % 2) * 4096:(n % 2 + 1) * 4096].rearrange("p (b d) -> p b d", d=128))
        kn_r, ki_r, vb_r, qn_r, qi_r, wi_r = self.pb_res
        pt_r = [P.res("PT") for _ in range(3)]
        sc_r = [P.res("score") for _ in range(8)]
        work_r, m8_r, maskb_r = P.res("work"), P.res("m8"), P.res("maskb")
        maskT_rs = [P.res("maskT") for _ in range(2)]
        mid_r, cnt_r, tt_r = P.res("mid"), P.res("cnt"), P.res("tt")
        for n in range(4):
            self.load(Kn[:, n, 0:nk], self.KTt[n, :, 0:nk], kn_r, self.d_qkv) if n == 0 else \
                P.dma("sp", (lambda n: (lambda e: e.dma_start(out=Kn[:, n, 0:nk], in_=self.KTt[n, :, 0:nk])))(n),
                      kn_r, self.d_qkv, kn_r, nowaw=True)
            P.dma("sp", (lambda n: (lambda e: e.dma_start(out=Vb[n][:, 0:nkb_all, :], in_=self.Vh[n, :, 0:nkb_all, :])))(n),
                  vb_r, self.d_qkv, vb_r, nowaw=(n > 0))
        self.load(Ki[:, 0:nk], self.KiT[:, 0:nk], ki_r, self.d_qkv)
        pst = {"pti": 0}

        def front(tl):
            i = (vs * VS) // 128 + tl
            t0 = i * 128
            q1 = t0 + 128
            nkb = i + 1
            pp = tl % 2
            self.load(Qn[pp], self.QT[:, :, t0:t0 + 128].rearrange("h p t -> p h t"), qn_r[pp], self.d_qkv)
            self.load(Qi[pp], self.QiT[:, :, t0:t0 + 128].rearrange("j p t -> p j t"), qi_r[pp], self.d_qkv)
            wi = self.wi_t[pp]
            self.load(wi[:], self.WI[t0:t0 + 128, :], wi_r[pp], self.d_qkv)
            nch = (q1 + 511) // 512
            for c in range(nch):
                c0 = c * 512
                wc = min(512, q1 - c0)
                for ih in range(16):
                    j, half = divmod(ih, 2)
                    Db, Dr = self.bank_from([0, 1, 2, 3], "D")
                    self.mm(Db[:, 0:wc], Qi[pp][half * 64:(half + 1) * 64, j, :], Ki[half * 64:(half + 1) * 64, c0:c0 + wc],
                            True, True, [qi_r[pp], ki_r], writes=[Dr])
                    R, Rr = self.ftmp()
                    self.act(R[:, 0:wc], Db[:, 0:wc], AF.Relu, [Dr], [Rr])
                    sc = score[:, c0:c0 + wc]
                    if ih == 0:
                        self.dve((lambda sc, R, wc, wi: (lambda e: e.tensor_scalar(
                            out=sc, in0=R[:, 0:wc], scalar1=wi[:, 0:1], scalar2=SHIFT, op0=ALU.mult, op1=ALU.add)))(sc, R, wc, wi),
                            [Rr, wi_r[pp]], [sc_r[c]])
                    else:
                        self.dve((lambda sc, R, wc, wi, ih: (lambda e: e.scalar_tensor_tensor(
                            out=sc, in0=R[:, 0:wc], scalar=wi[:, ih:ih + 1], in1=sc, op0=ALU.mult, op1=ALU.add)))(sc, R, wc, wi, ih),
                            [Rr, wi_r[pp], sc_r[c]], [sc_r[c]])
            cd = t0 // 512
            self.dve((lambda t0: (lambda e: e.tensor_tensor(out=score[:, t0:t0 + 128], in0=score[:, t0:t0 + 128],
                                                           in1=self.adm[:], op=ALU.mult)))(t0),
                     [sc_r[cd], self.c_r], [sc_r[cd]])
            if self.dbg:
                for c in range(nch):
                    c0 = c * 512; wc = min(512, q1 - c0)
                    self.store(self.SCd[t0:t0 + 128, c0:c0 + wc], score[:, c0:c0 + wc], sc_r[c], self.d_qkv)
            m8 = self.m8
            TK = self.TOPK
            if q1 <= TK:
                P.op("dve", lambda e: e.memset(m8[:, 7:8], 1.0), [], [m8_r])
            else:
                mid, cnt, tt = self.bis[:, 0:1], self.bis[:, 1:2], self.bis[:, 2:3]
                P.op("dve", lambda e: e.memset(mid, SHIFT), [], [mid_r])
                NR = 21
                for r in range(NR):
                    w_r = 64.0 * 2.0 ** (-r)
                    w_n = 64.0 * 2.0 ** (-(r + 1))
                    self.dve(lambda e: e.tensor_scalar(out=work[:, 0:q1], in0=score[:, 0:q1], scalar1=mid, scalar2=0.0,
                                                       op0=ALU.is_ge, op1=ALU.add, accum_out=cnt),
                             sc_r[0:nch] + [mid_r], [work_r, cnt_r])
                    self.dve((lambda w_r: (lambda e: e.tensor_scalar(out=tt, in0=cnt, scalar1=float(TK), scalar2=w_r,
                                                                     op0=ALU.is_ge, op1=ALU.mult)))(w_r), [cnt_r], [tt_r])
                    self.dve((lambda w_n: (lambda e: e.scalar_tensor_tensor(out=mid, in0=tt, scalar=-w_n, in1=mid,
                                                                            op0=ALU.add, op1=ALU.add)))(w_n),
                             [tt_r, mid_r], [mid_r])
                w_R = 64.0 * 2.0 ** (-NR)
                self.dve(lambda e: e.tensor_scalar(out=m8[:, 7:8], in0=mid, scalar1=-w_R, scalar2=None, op0=ALU.add),
                         [mid_r], [m8_r])
            self.dve(lambda e: e.tensor_scalar(out=maskb[:, 0:q1], in0=score[:, 0:q1], scalar1=m8[:, 7:8], scalar2=NEG,
                                               op0=ALU.is_lt, op1=ALU.mult), sc_r[0:nch] + [m8_r], [maskb_r])
            if self.dbg:
                self.store(self.MKd[t0:t0 + 128, 0:q1], maskb[:, 0:q1], maskb_r, self.d_qkv)
        def trans(tl):
            i = (vs * VS) // 128 + tl
            t0 = i * 128
            q1 = t0 + 128
            nkb = i + 1
            pp = tl % 2
            mT = maskTs[tl % 2]
            mT_r = maskT_rs[tl % 2]
            for kb0 in range(0, nkb, 8):
                n8 = min(8, nkb - kb0)
                tb, tbr = self.bank_from([0, 1, 2, 3], "D")
                tbv = tb[:].bitcast(BF16)
                for kk in range(n8):
                    kb = kb0 + kk
                    P.op("pe", (lambda tbv, kk, kb: (lambda e: e.transpose(tbv[:, kk * 128:(kk + 1) * 128],
                                                                          maskb[:, kb * 128:(kb + 1) * 128], self.identb[:])))(tbv, kk, kb),
                         [maskb_r, self.c_r], writes=[tbr] if kk == 0 else [], pwrites=[] if kk == 0 else [tbr],
                         signal=(kk == n8 - 1))
                dst = mT[:, kb0:kb0 + n8, :].rearrange("p b t -> p (b t)")
                self.evac(dst, tbv[:, 0:n8 * 128], [tbr], writes=[mT_r] if kb0 == 0 else [],
                          pwrites=[] if kb0 == 0 else [mT_r])
        def back(tl):
            i = (vs * VS) // 128 + tl
            t0 = i * 128
            q1 = t0 + 128
            nkb = i + 1
            pp = tl % 2
            mT = maskTs[tl % 2]
            mT_r = maskT_rs[tl % 2]
            for n in range(4):
                O, Or = self.bank_from([6, 7], "O")
                Dn, Dnr = self.bank_from([4, 5], "Dn")
                qs = Qn[pp][:, 4 * n:4 * n + 4, :]
                for kb in range(nkb):
                    ST, STr = self.bank_from([0, 1, 2, 3], "D")
                    near = kb >= i - 1
                    self.mm(ST[:], Kn[:, n, kb * 128:(kb + 1) * 128], qs, True, False, [kn_r, qn_r[pp]], writes=[STr],
                            signal=False)
                    for hh in range(4):
                        lastm = (hh == 3) and not near
                        self.mm(ST[:, hh * 128:(hh + 1) * 128], self.identb[:], mT[:, kb, :], False, lastm,
                                [mT_r, self.c_r], pwrites=[STr], signal=lastm)
                    if near:
                        which = 0 if kb == i else 1
                        self.mm(ST[:], self.identb[:], self.BT[:, which, 4 * n:4 * n + 4, :], False, True,
                                [self.bt_r, self.c_r], pwrites=[STr])
                    pt, ptr = PT[pst["pti"] % 3], pt_r[pst["pti"] % 3]
                    pst["pti"] += 1
                    self.act(pt, ST[:], AF.Exp, [STr], [ptr])
                    f0 = kb == 0
                    l0 = kb == nkb - 1
                    self.mm(O[:], Vb[n][:, kb, :], pt, f0, l0, [vb_r, ptr], writes=[Or] if f0 else [],
                            pwrites=[] if f0 else [Or], signal=False)
                    self.mm(Dn[:], self.onesb[:], pt, f0, l0, [ptr, self.c_r], writes=[Dnr] if f0 else [],
                            pwrites=[] if f0 else [Dnr])
                self.act(Dn[:], Dn[:], AF.Ln, [Dnr], [Dnr])
                rec, recr = self.ftmp()
                self.act(rec[:], Dn[:], AF.Exp, [Dnr], [recr], scale=-1.0)
                oc, ocr = self.ftmp()
                self.act(oc[:], O[:], AF.Copy, [Or], [ocr])
                dst = self.HT[:, 4 * n:4 * n + 4, tl * 128:(tl + 1) * 128]
                self.pool((lambda dst, oc, rec: (lambda e: e.tensor_tensor(
                    out=dst, in0=oc[:].rearrange("p (h t) -> p h t", h=4), in1=rec[:].rearrange("p (h t) -> p h t", h=4),
                    op=ALU.mult)))(dst, oc, rec), [ocr, recr], pwrites=[self.ht_r[tl // 4]])

        NT = VS // 128
        for tl in range(NT):
            front(tl)
            if tl >= 1:
                back(tl - 1)
            trans(tl)
        back(NT - 1)

    def build(self):
        self.setup()
        P = self.P
        self.epsb = self.sb("epsb", [128, 2], F32)
        P.op("pool", lambda e: e.memset(self.epsb[:, 0:1], EPS), [], [self.c_r])
        P.op("pool", lambda e: e.memset(self.epsb[:, 1:2], 1.0), [], [self.c_r])
        self.prologue()
        P.barrier()
        for (kind, mj, li) in self.layers:
            if kind is None:
                self.post_phase(li, None)
                P.barrier()
                continue
            for vs in range(self.NVS):
                groups = list(range(vs * self.GV, (vs + 1) * self.GV))
                if kind == "A":
                    self.p1_A(li, mj, vs)
                    P.barrier()
                    self.p2_A(vs)
                    P.barrier()
                    self.post_phase(li, self.w_out_a[mj], groups)
                else:
                    self.p1_B(li, mj, vs)
                    P.barrier()
                    self.p2_B(vs)
                    P.barrier()
                    if self.dbg:
                        self.store(self.OTd[:, :, vs * self.VS:(vs + 1) * self.VS], self.HT[:], self.ht_r[0], self.d_qkv)
                        P.barrier()
                    self.post_phase(li, self.w_out_b[mj], groups)
                P.barrier()
        self.epilogue()
        P.barrier()
        self.ninstr = {k: len(v) for k, v in P.ops.items()}
        P.emit()
        self.st.close()
        return self.nc


def make_inputs_common(inputs, layers):
    f = lambda a: np.ascontiguousarray(np.asarray(a, dtype=np.float32))
    common = {k: f(inputs[k]) for k in ("norm_mix", "q_norm_b", "k_norm_b", "rel_bias", "norm_mlp")}
    for (kind, mj, li) in layers:
        if kind == "A":
            common[f"w_in_a{mj}"] = f(inputs["w_in_a"][mj])
            common[f"w_out_a{mj}"] = f(inputs["w_out_a"][mj])
        elif kind == "B":
            common[f"w_in_b{mj}"] = f(inputs["w_in_b"][mj])
            common[f"w_out_b{mj}"] = f(inputs["w_out_b"][mj])
        common[f"w_up{li}"] = f(inputs["w_up"][li])
        common[f"w_down{li}"] = f(inputs["w_down"][li])
    common["onehot"] = _bucket_onehot()
    return common


def kernel(**inputs):
    x = np.asarray(inputs["x"], dtype=np.float32)
    B, S, _ = x.shape
    layers = [("A", 0, 0), ("B", 0, 1), ("A", 1, 2), ("B", 1, 3)]
    nc = Builder(S, layers).build()
    common = make_inputs_common(inputs, layers)
    in_maps = []
    for c in range(B):
        m = dict(common)
        m["x"] = np.ascontiguousarray(x[c])
        in_maps.append(m)
    res = run_bass_kernel_spmd(nc, in_maps, core_ids=list(range(B)))
    out = np.stack([res.results[b]["out"] for b in range(B)], axis=0)
    return out.astype(np.float32)
```
